# Optimizing a Trainium2 kernel written in Bass

```python
import jax, jax.numpy as jnp
from jax import lax
import numpy as np

D_MODEL = 1024
BATCH = 8
SEQ = 2048
DEPTH = 4
DEC_BATCH = 128
DEC_SEQ = 1
PAST_LEN = 16384
PAGE_SIZE = 128

A_HEADS = 4
A_DK = 128
A_DV = 128
A_WIDTH = A_HEADS * A_DK
A_VW = A_HEADS * A_DV
A_CHUNK = 64
B_WIDTH = D_MODEL // 4
B_GROUPS = 4
B_GDIM = B_WIDTH // B_GROUPS
POOL_WINDOWS = (2, 4, 8, 16)
POOL_STATE = max(POOL_WINDOWS) - 1
C_WIDTH = D_MODEL // 4
C_GROUPS = 4
C_GDIM = C_WIDTH // C_GROUPS
C_CHUNK = 128
N_BRANCH = 3
IN_SPLITS = tuple(np.cumsum([A_WIDTH, A_WIDTH, A_VW, A_VW, B_WIDTH, C_WIDTH, C_WIDTH]).tolist())
IN_COLS = 2 * A_WIDTH + 2 * A_VW + B_WIDTH + 2 * C_WIDTH + N_BRANCH * D_MODEL
D_FF = -(-8 * D_MODEL // (3 * 256)) * 256
EPS = 1e-6
LB_FLOOR = 1e-30

kernel_name = "hybrid_hgrn2_pool_gmlp_decode_step"


def rmsnorm(x):
    xf = x.astype(jnp.float32)
    return (xf * lax.rsqrt(jnp.mean(xf * xf, -1, keepdims=True) + EPS)).astype(x.dtype)


def layernorm(x, g, b):
    xf = x.astype(jnp.float32)
    mu = jnp.mean(xf, -1, keepdims=True)
    var = jnp.mean(jnp.square(xf - mu), -1, keepdims=True)
    return ((xf - mu) * lax.rsqrt(var + EPS)).astype(x.dtype) * g + b


def hgrn2_recurrence(q, log_f, k, v, s0):
    B, L, H = q.shape[:3]
    cl = min(A_CHUNK, L)
    n = -(-L // cl)
    pad = n * cl - L

    def prep(a):
        a = jnp.pad(a.astype(jnp.float32), ((0, 0), (0, pad), (0, 0), (0, 0)))
        return a.reshape(B, n, cl, H, a.shape[-1]).transpose(1, 0, 3, 2, 4)

    qc, gc, kc, vc = prep(q), prep(log_f), prep(k), prep(v)
    causal = jnp.tril(jnp.ones((cl, cl), bool))[:, :, None]

    def step(s, inp):
        qb, gb, kb, vb = inp
        b = jnp.cumsum(gb, axis=2)
        diff = b[:, :, :, None, :] - b[:, :, None, :, :]
        decay = jnp.where(causal, jnp.exp(jnp.where(causal, diff, 0.0)), 0.0)
        scores = jnp.einsum('bhtk,bhtsk,bhsk->bhts', qb, decay, kb)
        o = (jnp.einsum('bhts,bhsv->bhtv', scores, vb)
             + jnp.einsum('bhtk,bhkv->bhtv', qb * jnp.exp(b), s))
        b_last = b[:, :, -1:, :]
        s_new = (jnp.exp(b_last[:, :, 0, :])[..., None] * s
                 + jnp.einsum('bhsk,bhsv->bhkv', kb * jnp.exp(b_last - b), vb))
        return s_new, o

    s_fin, o = lax.scan(step, s0.astype(jnp.float32), (qc, gc, kc, vc))
    o = o.transpose(1, 0, 3, 2, 4).reshape(B, n * cl, H, vc.shape[-1])[:, :L]
    return o, s_fin


def pool_mixer(p, prefix, n_valid, w_map, scale):
    Bn, L = p.shape[:2]
    xp = jnp.concatenate([prefix.astype(p.dtype), p], 1)
    xf = xp.astype(jnp.float32)
    cs = jnp.pad(jnp.cumsum(xf, 1), ((0, 0), (1, 0), (0, 0)))
    j = jnp.arange(L)
    end = cs[:, POOL_STATE + 1:]
    means = []
    for g, w in enumerate(POOL_WINDOWS):
        sl = slice(g * B_GDIM, (g + 1) * B_GDIM)
        start = cs[:, POOL_STATE + 1 - w: POOL_STATE + 1 - w + L, sl]
        cnt = jnp.minimum(j + 1 + n_valid, w).astype(jnp.float32)[None, :, None]
        means.append((end[..., sl] - start) / cnt)
    z = (jnp.concatenate(means, -1) - xf[:, POOL_STATE:]).astype(p.dtype)
    z = jnp.einsum('blgc,gcd->blgd', z.reshape(Bn, L, B_GROUPS, B_GDIM), w_map)
    return z.reshape(Bn, L, B_WIDTH) * scale, xp[:, -POOL_STATE:]


def chunk_spatial_gating(u, v, ln_g, ln_b, w_s, b_s):
    Bn, L = u.shape[:2]
    vn = layernorm(v, ln_g, ln_b)
    cl = min(C_CHUNK, L)
    n = -(-L // cl)
    pad = n * cl - L
    vp = jnp.pad(vn, ((0, 0), (0, pad), (0, 0))).reshape(Bn, n, cl, C_GROUPS, C_GDIM)
    w = jnp.where(jnp.tril(jnp.ones((cl, cl), bool)), w_s[:, :cl, :cl], 0.0)
    s = jnp.einsum('gts,bnsgc->bntgc', w, vp) + b_s[:, :cl].T[None, None, :, :, None]
    s = s.reshape(Bn, n * cl, C_WIDTH)[:, :L]
    return u * s, vn


def trunk(x, c, st_hgrn, st_pool, n_valid, keep_chunk_rows, w_ada, b_ada, w_in, lb_logits, a_norm_g,
          pool_map, pool_scale, c_ln_g, c_ln_b, c_ws, c_bs, w_br_a, w_br_b, w_br_c, w_out,
          w_ffn_in, w_ffn_out, final_g):
    Bn, L, _ = x.shape
    p_lb = jax.nn.softmax(lb_logits.astype(jnp.float32), axis=0)
    lower_bounds = jnp.maximum(jnp.cumsum(p_lb, axis=0) - p_lb[0:1], 0.0)
    cond = jax.nn.silu(c)
    new_h, new_p, new_v = [], [], []
    for l in range(DEPTH):
        mod = (cond @ w_ada[l] + b_ada[l])[:, None, :]
        sh1, sc1, g1, sh2, sc2, g2 = jnp.split(mod, 6, axis=-1)
        h = rmsnorm(x) * (1 + sc1) + sh1
        z = h @ w_in[l]
        q, f_pre, i_a, g_a, p_b, u_c, v_c, gates = jnp.split(z, IN_SPLITS, axis=-1)
        lb = lower_bounds[l]
        f32 = f_pre.astype(jnp.float32)
        log_f = jnp.logaddexp(jnp.log(lb + LB_FLOOR), jnp.log1p(-lb) + jax.nn.log_sigmoid(f32))
        k = (1 - lb) * jax.nn.sigmoid(-f32)
        o, s_fin = hgrn2_recurrence(
            jax.nn.silu(q).reshape(Bn, L, A_HEADS, A_DK), log_f.reshape(Bn, L, A_HEADS, A_DK),
            k.reshape(Bn, L, A_HEADS, A_DK), i_a.reshape(Bn, L, A_HEADS, A_DV), st_hgrn[l])
        o = rmsnorm(o.astype(x.dtype)) * a_norm_g[l] * jax.nn.silu(g_a.reshape(Bn, L, A_HEADS, A_DV))
        y_a = o.reshape(Bn, L, A_VW) @ w_br_a[l]
        pb, pool_rows = pool_mixer(p_b, st_pool[l], n_valid, pool_map[l], pool_scale[l])
        y_b = pb @ w_br_b[l]
        gm, vn = chunk_spatial_gating(jax.nn.gelu(u_c), jax.nn.gelu(v_c), c_ln_g[l], c_ln_b[l], c_ws[l], c_bs[l])
        y_c = gm @ w_br_c[l]
        ga, gb, gc = jnp.split(jax.nn.sigmoid(gates), N_BRANCH, axis=-1)
        m = ga * y_a + gb * y_b + gc * y_c
        x = x + g1 * (m @ w_out[l])
        h2 = rmsnorm(x) * (1 + sc2) + sh2
        gt, up = jnp.split(h2 @ w_ffn_in[l], 2, axis=-1)
        x = x + g2 * ((jax.nn.silu(gt) * up) @ w_ffn_out[l])
        new_h.append(s_fin.astype(x.dtype))
        new_p.append(pool_rows)
        if keep_chunk_rows:
            new_v.append(vn)
    y = rmsnorm(x) * final_g
    v_rows = jnp.stack(new_v) if keep_chunk_rows else None
    return y, jnp.stack(new_h), jnp.stack(new_p), v_rows


def setup_inputs(seed: int = 0) -> dict:
    key = jax.random.key(seed)
    ks = jax.random.split(key, 32)
    nrm = lambda k, shape, s: jax.random.normal(k, shape, jnp.float32) * s
    return {
        'x_prompt': nrm(ks[0], (BATCH, SEQ, D_MODEL), 1.0),
        'x_sample': nrm(ks[1], (DEC_BATCH, DEC_SEQ, D_MODEL), 1.0),
        'state_hgrn': nrm(ks[2], (DEPTH, DEC_BATCH, A_HEADS, A_DK, A_DV), 0.5),
        'state_pool': nrm(ks[3], (DEPTH, DEC_BATCH, POOL_STATE, B_WIDTH), 1.0),
        'c_prompt': nrm(ks[4], (BATCH, D_MODEL), 1.0),
        'c_sample': nrm(ks[5], (DEC_BATCH, D_MODEL), 1.0),
        'w_ada': nrm(ks[6], (DEPTH, D_MODEL, 6 * D_MODEL), 0.5 * D_MODEL ** -0.5),
        'b_ada': nrm(ks[7], (DEPTH, 6 * D_MODEL), 0.02),
        'w_in': nrm(ks[8], (DEPTH, D_MODEL, IN_COLS), D_MODEL ** -0.5),
        'lb_logits': nrm(ks[9], (DEPTH, A_WIDTH), 1.0),
        'a_norm_g': 1.0 + nrm(ks[10], (DEPTH, A_HEADS, A_DV), 0.05),
        'pool_map': nrm(ks[11], (DEPTH, B_GROUPS, B_GDIM, B_GDIM), B_GDIM ** -0.5),
        'pool_scale': 1.0 + nrm(ks[12], (DEPTH, B_WIDTH), 0.05),
        'c_ln_g': 1.0 + nrm(ks[13], (DEPTH, C_WIDTH), 0.05),
        'c_ln_b': nrm(ks[14], (DEPTH, C_WIDTH), 0.02),
        'c_ws': nrm(ks[15], (DEPTH, C_GROUPS, C_CHUNK, C_CHUNK), C_CHUNK ** -0.5),
        'c_bs': 1.0 + nrm(ks[16], (DEPTH, C_GROUPS, C_CHUNK), 0.1),
        'w_br_a': nrm(ks[17], (DEPTH, A_VW, D_MODEL), A_VW ** -0.5),
        'w_br_b': nrm(ks[18], (DEPTH, B_WIDTH, D_MODEL), B_WIDTH ** -0.5),
        'w_br_c': nrm(ks[19], (DEPTH, C_WIDTH, D_MODEL), C_WIDTH ** -0.5),
        'w_out': nrm(ks[20], (DEPTH, D_MODEL, D_MODEL), D_MODEL ** -0.5),
        'w_ffn_in': nrm(ks[21], (DEPTH, D_MODEL, 2 * D_FF), D_MODEL ** -0.5),
        'w_ffn_out': nrm(ks[22], (DEPTH, D_FF, D_MODEL), D_FF ** -0.5),
        'final_g': 1.0 + nrm(ks[23], (D_MODEL,), 0.05),
    }


def reference(x_prompt, x_sample, state_hgrn, state_pool, c_prompt, c_sample, w_ada, b_ada, w_in,
              lb_logits, a_norm_g, pool_map, pool_scale, c_ln_g, c_ln_b, c_ws, c_bs, w_br_a, w_br_b,
              w_br_c, w_out, w_ffn_in, w_ffn_out, final_g):
    weights = (w_ada, b_ada, w_in, lb_logits, a_norm_g, pool_map, pool_scale, c_ln_g, c_ln_b, c_ws, c_bs,
               w_br_a, w_br_b, w_br_c, w_out, w_ffn_in, w_ffn_out, final_g)
    Bp = x_prompt.shape[0]
    h0 = jnp.zeros((DEPTH, Bp, A_HEADS, A_DK, A_DV), x_prompt.dtype)
    p0 = jnp.zeros((DEPTH, Bp, POOL_STATE, B_WIDTH), x_prompt.dtype)
    y_prompt, hgrn_prompt, pool_prompt, _ = trunk(x_prompt, c_prompt, h0, p0, 0, False, *weights)
    y_sample, hgrn_sample, pool_sample, chunk_v_sample = trunk(
        x_sample, c_sample, state_hgrn, state_pool, min(POOL_STATE, PAST_LEN), True, *weights)
    return (y_prompt, y_sample, hgrn_prompt, pool_prompt, hgrn_sample, pool_sample, chunk_v_sample)
```

```python
import contextlib
import numpy as np
import concourse.bass as bass
import concourse.mybir as mybir
from concourse.bass_utils import run_bass_kernel_spmd

F32 = mybir.dt.float32
BF16 = mybir.dt.bfloat16
U8 = mybir.dt.uint8
AF = mybir.ActivationFunctionType
ALU = mybir.AluOpType
AX = mybir.AxisListType

SEM_EPOCH = 30000
SAME_ENGINE_SYNC = True

D = 1024
NPT = 2048
NS = 16
NT = NPT + NS
DEPTH = 4
IN_COLS = 5888
DFF = 2816
EPS = 1e-6
NW = 6
WSLOT = 2048
FFN_ROUNDS = [(0, 4), (4, 4), (8, 4), (12, 4), (16, 3), (19, 3)]
NCONST = 546


class Sched:
    ENG = ("pe", "act", "dve", "pool", "sp")

    def __init__(self, nc, stack):
        self.nc = nc
        self.stack = stack
        self.prog = {e: [] for e in self.ENG}
        self.cur = {e: None for e in self.ENG}
        self.seen = {e: {} for e in self.ENG}
        self.lastw = {}
        self.readers = {}
        self.nsem = 0
        self.slot_sems = {}
        self.ninst = {e: 0 for e in self.ENG}
        self.dead = set()

    def new_sem(self, name):
        self.nsem += 1
        return self.stack.enter_context(self.nc.semaphore(f"{name}_{self.nsem}"))

    def _wait(self, e, tok):
        if tok is None:
            return
        sem, val, owner = tok
        if owner == e and not SAME_ENGINE_SYNC:
            return
        if self.seen[e].get(sem, 0) >= val:
            return
        self.seen[e][sem] = val
        self.prog[e].append(lambda eng, sem=sem, val=val: eng.wait_ge(sem, val))

    def retire(self, olds, new):
        if not isinstance(olds, list):
            olds = [olds]
        toks = []
        for old in olds:
            lw = self.lastw.pop(old, None)
            if lw is not None:
                toks.append(lw)
            toks += self.readers.pop(old, [])
            self.dead.add(old)
        self.readers[new] = toks

    def _deps(self, e, reads, writes, own_sem=None):
        for k in list(reads) + list(writes):
            assert k not in self.dead, f"stale buffer key used after re-allocation: {k}"
        need = {}

        def want(tok):
            if tok is None:
                return
            sem, val, owner = tok
            if own_sem is not None and sem is own_sem:
                return
            if sem not in need or need[sem][1] < val:
                need[sem] = tok
        for k in reads:
            want(self.lastw.get(k))
            if k in ("psl", "psb") or (isinstance(k, tuple) and k and k[0] == "ps"):
                for t in self.readers.get(k, ()):
                    if t[2] != e:
                        want(t)
        for k in writes:
            want(self.lastw.get(k))
            for t in self.readers.get(k, ()):
                want(t)
        for tok in need.values():
            self._wait(e, tok)

    def _commit(self, tok, reads, writes):
        for k in reads:
            self.readers.setdefault(k, []).append(tok)
        for k in writes:
            self.lastw[k] = tok
            self.readers[k] = []

    def _eng_token(self, e):
        c = self.cur[e]
        if c is None or c[1] >= SEM_EPOCH:
            c = [self.new_sem("s" + e), 0]
            self.cur[e] = c
        c[1] += 1
        return (c[0], c[1], e)

    def op(self, e, fns, reads=(), writes=()):
        if callable(fns):
            fns = [fns]
        self._deps(e, reads, writes)
        tok = self._eng_token(e)
        n = len(fns)
        for i, fn in enumerate(fns):
            if i == n - 1:
                self.prog[e].append(lambda eng, fn=fn, tok=tok: fn(eng).then_inc(tok[0], 1))
            else:
                self.prog[e].append(lambda eng, fn=fn: fn(eng))
        self.ninst[e] += n
        self._commit(tok, reads, writes)
        return tok

    def dma(self, q, out, in_, reads=(), writes=(), slot=None, **kw):
        if slot is None:
            slot = ("auto", tuple(writes) if writes else tuple(reads))
        s = self.slot_sems.get(slot)
        if s is None or s[1] + 16 > SEM_EPOCH:
            s = [self.new_sem("d"), 0]
            self.slot_sems[slot] = s
        self._deps(q, reads, writes, own_sem=(s[0] if writes else None))
        s[1] += 16
        tok = (s[0], s[1], "dma")
        self.prog[q].append(lambda eng, tok=tok, out=out, in_=in_, kw=kw:
                            eng.dma_start(out=out, in_=in_, **kw).then_inc(tok[0], 16))
        self.ninst[q] += 1
        self._commit(tok, reads, writes)
        return tok

    def barrier(self, engines=("pe", "act", "dve", "sp")):
        toks = []
        for e in self.ENG:
            c = self.cur[e]
            if c is not None and c[1] > 0:
                toks.append((c[0], c[1], e))
        for s in self.slot_sems.values():
            toks.append((s[0], s[1], "dma"))
        for e in engines:
            for t in toks:
                if t[2] == e:
                    continue
                self._wait(e, t)

    def wait_all(self, e="sp"):
        for k, t in list(self.lastw.items()):
            self._wait(e, t)
        for k, ts in list(self.readers.items()):
            for t in ts:
                self._wait(e, t)

    def emit(self):
        nc = self.nc
        with nc.Block() as block:
            @block.tensor
            def _(eng):
                for t in self.prog["pe"]:
                    t(eng)

            @block.scalar
            def _(eng):
                for t in self.prog["act"]:
                    t(eng)

            @block.vector
            def _(eng):
                for t in self.prog["dve"]:
                    t(eng)

            @block.gpsimd
            def _(eng):
                for t in self.prog["pool"]:
                    t(eng)

            @block.sync
            def _(eng):
                for t in self.prog["sp"]:
                    t(eng)


class Ring:
    sched = None

    def __init__(self, name, aps):
        self.name = name
        self.aps = aps
        self.i = 0
        self.gen = [0] * len(aps)

    def next(self):
        i = self.i % len(self.aps)
        self.i += 1
        old = (self.name, i, self.gen[i])
        self.gen[i] += 1
        new = (self.name, i, self.gen[i])
        Ring.sched.retire([old] + [(old, j) for j in range(2)], new)
        return self.aps[i], new


def make_consts():
    c = np.zeros((128, NCONST), np.float32)
    c[:, 0:128] = np.eye(128, dtype=np.float32)
    s = np.arange(128)[:, None]
    t = np.arange(128)[None, :]
    c[:, 128:256] = ((s // 64 == t // 64) & (s <= t)).astype(np.float32)
    c[:, 256:384] = (s <= t).astype(np.float32)
    w_of = lambda p, cc: [2, 4, 8, 16][2 * cc + p // 64]
    for p in range(128):
        for cc in range(2):
            w = w_of(p, cc)
            for tt in range(16):
                c[p, 384 + cc * 16 + tt] = 1.0 / min(tt + 1, w)
            c[p, 416 + cc] = 1.0 / w
    for g in range(4):
        w = [2, 4, 8, 16][g]
        for t in range(2):
            for bb in range(8):
                for r in range(15):
                    if r >= 16 - w:
                        c[bb * 15 + r, 418 + (g * 2 + t) * 16 + bb + 8 * t] = 1.0
    return c


def build(nl=DEPTH, debug=False):
    nc = bass.Bass("TRN2", target_bir_lowering=False)
    din = lambda name, shape: nc.dram_tensor(name, list(shape), F32, kind="ExternalInput").ap()
    dout = lambda name, shape: nc.dram_tensor(name, list(shape), F32, kind="ExternalOutput").ap()
    xp = din("xp", [NPT, D])
    xs = din("xs", [NS, D])
    sh = din("sh", [DEPTH, NS, 4, 128, 128])
    spl = din("spl", [DEPTH, NS, 15, 256])
    cc = din("cc", [17, D])
    w_ada = din("w_ada", [DEPTH, D, 6 * D])
    b_ada = din("b_ada", [DEPTH, 6 * D])
    w_in = din("w_in", [DEPTH, D, IN_COLS])
    lb_logits = din("lb_logits", [DEPTH, 512])
    a_norm_g = din("a_norm_g", [DEPTH, 4, 128])
    pool_map = din("pool_map", [DEPTH, 4, 64, 64])
    pool_scale = din("pool_scale", [DEPTH, 256])
    c_ln_g = din("c_ln_g", [DEPTH, 256])
    c_ln_b = din("c_ln_b", [DEPTH, 256])
    c_ws = din("c_ws", [DEPTH, 4, 128, 128])
    c_bs = din("c_bs", [DEPTH, 4, 128])
    w_br_a = din("w_br_a", [DEPTH, 512, D])
    w_br_b = din("w_br_b", [DEPTH, 256, D])
    w_br_c = din("w_br_c", [DEPTH, 256, D])
    w_out = din("w_out", [DEPTH, D, D])
    w_ffn_in = din("w_ffn_in", [DEPTH, D, 2 * DFF])
    w_ffn_out = din("w_ffn_out", [DEPTH, DFF, D])
    final_g = din("final_g", [8, 128])
    consts = din("consts", [128, NCONST])

    y_p = dout("y_p", [NPT, D])
    y_s = dout("y_s", [NS, D])
    hg_p = dout("hg_p", [DEPTH, 4, 128, 128])
    pl_p = dout("pl_p", [DEPTH, 15, 256])
    hg_s = dout("hg_s", [DEPTH, NS, 4, 128, 128])
    pl_s = dout("pl_s", [DEPTH, NS, 15, 256])
    cv_s = dout("cv_s", [DEPTH, NS, 256])
    dbg = {}
    if debug:
        dbg["x"] = dout("dbg_x", [DEPTH + 1, 128, 8, NT])
        dbg["m"] = dout("dbg_m", [DEPTH, 128, 48, 17])

    with contextlib.ExitStack() as st:
        S = Sched(nc, st)
        Ring.sched = S
        sb = lambda name, shape, dt=F32: st.enter_context(nc.sbuf_tensor(name, list(shape), dt))
        psum = lambda name, shape, dt=F32: st.enter_context(nc.psum_tensor(name, list(shape), dt))

        xT = sb("xT", [128, 8, NT])
        AB = sb("arena_b", [128, 25600], BF16)
        AFt = sb("arena_f", [128, 3072])
        WR = [sb(f"wr{i}", [128, WSLOT], BF16) for i in range(NW)]
        modTs = [sb(f"modT{i}", [128, 48, 17]) for i in range(2)]
        modT, MK = modTs[0], ("modT", 0)
        condT = sb("condT", [128, 8, 17], BF16)
        ident = sb("ident", [128, 128])
        identb = sb("identb", [128, 128], BF16)
        onesb = sb("onesb", [128, 128], BF16)
        scmask = sb("scmask", [128, 4, 128], U8)
        trimask = sb("trimask", [128, 4, 128], U8)
        selw = sb("selw", [128, 128])
        rmask = sb("rmask", [128, 512])
        zeros = sb("zeros", [128, 128])
        rc16 = sb("rc16", [128, 2, 16])
        invw = sb("invw", [128, 2])
        LBv = sb("LBv", [128, 4, 4])
        OML = sb("OML", [128, 4, 4])
        NOML = sb("NOML", [128, 4, 4])
        LBB = sb("LBB", [128, 4, 4])
        ANG = sb("ANG", [128, 16])
        FG = sb("FG", [128, 8])
        PSC = sb("PSC", [128, 8])
        BADAs = [sb(f"BADA{i}", [128, 48]) for i in range(2)]
        bsT = sb("bsT", [128, 2, 128])
        w00 = sb("w00", [128, 2])
        b00 = sb("b00", [128, 2])
        lng = sb("lng", [128, 256])
        lnb = sb("lnb", [128, 256])
        BD = sb("BD", [128, 2, 128], BF16)
        WmT = sb("WmT", [128, 4, 128], BF16)
        Sst = sb("Sst", [128, 4, 128])
        Sb = [sb(f"Sb{i}", [128, 4, 128], BF16) for i in range(2)]
        EB = sb("EB", [128, 4, 8])
        RF = Ring("rf", [sb(f"rf{i}", [128, 512]) for i in range(4)])
        RS = Ring("rs", [sb(f"rs{i}", [128, 512]) for i in range(2)])
        RB = Ring("rb", [sb(f"rbf{i}", [128, 512], BF16) for i in range(2)])
        QS = [sb(f"qs{i}", [128, 512]) for i in range(1)]
        KK = [sb(f"kk{i}", [128, 512]) for i in range(1)]
        BB = [sb(f"bb{i}", [128, 512]) for i in range(1)]
        MACC = sb("macc", [128, 512])
        hTs = sb("hTs", [128, 8, NS], BF16)
        Vs = sb("Vs", [NS, 512])
        vns = sb("vns", [NS, 256])
        xnf = sb("xnf", [128, 2, NS])
        sS0 = [sb(f"sS0{i}", [128, 4, 128]) for i in range(2)]
        qs_s = sb("qs_s", [128, 4, NS])
        f_s = sb("f_s", [128, 4, NS])
        kk_s = sb("kk_s", [128, 4, NS])
        GSs = sb("GSs", [128, 4, NS])
        OTs = sb("OTs", [128, 4, NS], BF16)
        ZBs = sb("ZBs", [128, 2, NS], BF16)
        PBMs = sb("PBMs", [128, 2, NS], BF16)
        us_s = sb("us_s", [128, 2, NS])
        GMs = sb("GMs", [128, 2, NS], BF16)
        MTs = sb("MTs", [128, 8, NS], BF16)
        maccs = sb("maccs", [128, NS])
        tmps = sb("tmps", [128, 8, NS])
        tail = sb("tail", [NS, 256])
        stat = sb("stat", [128, 24])
        NEGH = sb("NEGH", [128, 1])

        o = 0
        def carve(n):
            nonlocal o
            v = AB[:, o:o + n]
            o += n
            return v
        hTg = carve(8 * 512).rearrange("p (k t) -> p k t", t=512)
        QX = carve(8 * 512).rearrange("p (h t) -> p h t", t=512)
        QH = QX[:, 0:4, :]
        QT = QX[:, 4:8, :]
        MT = QX
        KT = carve(4 * 512).rearrange("p (h t) -> p h t", t=512)
        KHtok = carve(4 * 512).rearrange("p (h s k) -> p h s k", s=4, k=128)
        Vtok = carve(4 * 512).rearrange("p (s c) -> p s c", c=512)
        GS = carve(4 * 512).rearrange("p (h t) -> p h t", t=512)
        OT = carve(4 * 512).rearrange("p (h t) -> p h t", t=512)
        ZB = carve(2 * 512).rearrange("p (c t) -> p c t", t=512)
        PBM = carve(2 * 512).rearrange("p (c t) -> p c t", t=512)
        VN = carve(4 * 256).rearrange("p (s c) -> p s c", c=256)
        GM = carve(2 * 512).rearrange("p (c t) -> p c t", t=512)
        KHTb = carve(512)
        scT = [carve(512).rearrange("p (h t) -> p h t", t=128) for _ in range(3)]
        assert o <= 25600, o
        h2T = AB[:, 0:8 * NT].rearrange("p (k t) -> p k t", t=NT)
        ACTB = AB[:, 8 * NT:12 * NT].rearrange("p (j t) -> p j t", t=NT)
        assert 12 * NT <= 25600
        PBt = AFt[:, 0:1054].rearrange("p (c t) -> p c t", t=527)
        PA = AFt[:, 1054:1581]
        PBb = AFt[:, 1581:2108]
        xin = [AFt[:, 1024:2048], AFt[:, 2048:3072]]
        smallin = AFt[:, 0:512]
        Xa = AFt[:, 2108:2364]
        Xb = AFt[:, 2364:2620]
        pnew = AFt[0:NS, 2620:2876]
        RSN = sb("RSN", [128, 512])

        PSF = [psum(f"psf{i}", [128, 512]) for i in range(6)]
        PSL = psum("psl", [128, 512])
        PSB = psum("psb", [128, 1024], BF16)
        psi = [0]

        def MM(out, lhsT, rhs, start=True, stop=True, skip=False):
            return lambda e: e.matmul(out, lhsT=lhsT, rhs=rhs, start=start, stop=stop, skip_group_check=skip)

        def SEL(out, mask, on_true, on_false):
            return lambda e: e.select(out=out, mask=mask, on_true=on_true, on_false=on_false, add_drain=True)

        pgen = [0] * 6

        def bank():
            i = psi[0] % 6
            psi[0] += 1
            old = ("ps", i, pgen[i])
            pgen[i] += 1
            new = ("ps", i, pgen[i])
            S.retire(old, new)
            return PSF[i], new

        wslot_i = [0]

        def wload(pieces):
            i = wslot_i[0] % NW
            wslot_i[0] += 1
            keys = [("w", i, j) for j in range(6)]
            views = []
            off = 0
            for j, src in enumerate(pieces):
                rows, cols = src.shape
                kch = rows // 128
                v = WR[i][:, off:off + kch * cols].rearrange("p (k n) -> p k n", n=cols)
                S.dma("pool", v, src.rearrange("(k p) n -> p k n", p=128), reads=(), writes=[keys[j]], slot=("w", i))
                views.append(v)
                off += kch * cols
            assert off <= WSLOT
            return views, keys

        def mm_group(out, pairs, reads, writes, extra=()):
            n = len(pairs)
            fns = [MM(out, l, r, i == 0, i == n - 1) for i, (l, r) in enumerate(pairs)]
            return S.op("pe", fns, reads=reads, writes=writes)

        def pe_T(out_ps, in_sb, idn, reads, writes):
            return S.op("pe", lambda e: e.transpose(out=out_ps, in_=in_sb, identity=idn), reads=reads, writes=writes)

        def act(out, in_, func, reads, writes, **kw):
            return S.op("act", lambda e: e.activation(out=out, in_=in_, func=func, **kw), reads=reads, writes=writes)

        def tt(out, in0, in1, op, reads, writes, eng="dve"):
            return S.op(eng, lambda e: e.tensor_tensor(out=out, in0=in0, in1=in1, op=op), reads=reads, writes=writes)

        def ts(out, in0, s1, s2, op0, op1, reads, writes, eng="dve"):
            if s2 is None:
                return S.op(eng, lambda e: e.tensor_scalar(out=out, in0=in0, scalar1=s1, scalar2=None, op0=op0), reads=reads, writes=writes)
            return S.op(eng, lambda e: e.tensor_scalar(out=out, in0=in0, scalar1=s1, scalar2=s2, op0=op0, op1=op1), reads=reads, writes=writes)

        def stt(out, in0, scalar, in1, op0, op1, reads, writes, eng="dve"):
            return S.op(eng, lambda e: e.scalar_tensor_tensor(out=out, in0=in0, scalar=scalar, in1=in1, op0=op0, op1=op1), reads=reads, writes=writes)

        def cp(out, in_, reads, writes, eng="dve"):
            if eng == "act":
                return act(out, in_, AF.Copy, reads, writes)
            return S.op(eng, lambda e: e.tensor_copy(out=out, in_=in_), reads=reads, writes=writes)

        def memset(ap, val, writes, eng="dve"):
            return S.op(eng, lambda e: e.memset(ap, val), writes=writes)

        def recip(out, in_, reads, writes):
            return S.op("dve", lambda e: e.reciprocal(out=out, in_=in_), reads=reads, writes=writes)

        def small_T(src_rows_ap, nrows, dst, dst_key, name):
            S.dma("sp", smallin[0:nrows, 0:128], src_rows_ap, writes=["PBt"], slot="PBt")
            pb, pk = bank()
            pe_T(pb[:, 0:nrows], smallin[0:nrows, 0:128], ident[0:nrows, 0:nrows], ["PBt", "ident"], [pk])
            cp(dst, pb[:, 0:nrows], [pk], [dst_key])

        xkeys = lambda ti: [("x", k, ti) for k in range(8)]

        S.dma("sp", ident[:], consts[:, 0:128], writes=["ident"])
        S.dma("sp", smallin[:, 0:256], consts[:, 128:384], writes=["PBt"], slot="PBt")
        S.dma("sp", selw[:], consts[:, 418:546], writes=["selw"])
        S.dma("sp", rc16[:], consts[:, 384:416].rearrange("p (c t) -> p c t", t=16), writes=["rc16"])
        S.dma("sp", invw[:], consts[:, 416:418], writes=["invw"])
        cp(identb[:], ident[:], ["ident"], ["identb"])
        memset(onesb[:], 1.0, ["onesb"])
        memset(zeros[:], 0.0, ["zeros"])
        memset(NEGH[:], -0.5, ["NEGH"])
        memset(rmask[:], 1.0, ["rmask"])
        memset(rmask[:].rearrange("p (c t) -> p c t", t=64)[:, :, 0:1], 0.0, ["rmask"])
        cp(scmask[:], smallin[:, 0:128].unsqueeze(1).to_broadcast([128, 4, 128]), ["PBt"], ["scmask"])
        cp(trimask[:], smallin[:, 128:256].unsqueeze(1).to_broadcast([128, 4, 128]), ["PBt"], ["trimask"])
        memset(BD[:], 0.0, ["BD"])

        S.dma("sp", xin[1][0:17, :], cc, writes=[("xin", 1)], slot=("xin", 1))
        act(xin[1][0:17, :], xin[1][0:17, :], AF.Silu, [("xin", 1)], [("xin", 1)])
        pb, pk = bank()
        for k in range(8):
            pe_T(pb[:, k * 17:(k + 1) * 17], xin[1][0:17, k * 128:(k + 1) * 128], ident[0:17, 0:17], [("xin", 1), "ident"], [pk])
        cp(condT[:], pb[:, 0:8 * 17].rearrange("p (k t) -> p k t", t=17), [pk], ["condT"])
        small_T(a_norm_g.rearrange("l h v -> (l h) v"), 16, ANG[:], "ANG", "ang")
        small_T(final_g, 8, FG[:], "FG", "fg")
        small_T(pool_scale.rearrange("l (c p) -> (l c) p", p=128), 8, PSC[:], "PSC", "psc")
        S.dma("sp", smallin[0:4, 0:512], lb_logits, writes=["PBt"], slot="PBt")
        pb, pk = bank()
        for h in range(4):
            pe_T(pb[:, h * 4:(h + 1) * 4], smallin[0:4, h * 128:(h + 1) * 128], ident[0:4, 0:4], ["PBt", "ident"], [pk])
        lg = OML
        cp(lg[:], pb[:, 0:16].rearrange("p (h l) -> p h l", l=4), [pk], ["OML"])
        S.op("dve", lambda e: e.tensor_reduce(out=stat[:, 0:4], in_=lg[:], axis=AX.X, op=ALU.max), reads=["OML"], writes=["stat"])
        tt(lg[:], lg[:], stat[:, 0:4].unsqueeze(2).to_broadcast([128, 4, 4]), ALU.subtract, ["OML", "stat"], ["OML"])
        act(lg[:], lg[:], AF.Exp, ["OML"], ["OML"])
        S.op("dve", lambda e: e.tensor_reduce(out=stat[:, 4:8], in_=lg[:], axis=AX.X, op=ALU.add), reads=["OML"], writes=["stat"])
        recip(stat[:, 4:8], stat[:, 4:8], ["stat"], ["stat"])
        tt(lg[:], lg[:], stat[:, 4:8].unsqueeze(2).to_broadcast([128, 4, 4]), ALU.mult, ["OML", "stat"], ["OML"])
        memset(LBv[:], 0.0, ["LBv"])
        cp(LBv[:, :, 1:2], lg[:, :, 1:2], ["OML"], ["LBv"])
        tt(LBv[:, :, 2:3], LBv[:, :, 1:2], lg[:, :, 2:3], ALU.add, ["OML", "LBv"], ["LBv"])
        tt(LBv[:, :, 3:4], LBv[:, :, 2:3], lg[:, :, 3:4], ALU.add, ["OML", "LBv"], ["LBv"])
        ts(LBv[:], LBv[:], 0.0, None, ALU.max, None, ["LBv"], ["LBv"])
        ts(LBB[:], LBv[:], 1e-30, None, ALU.add, None, ["LBv"], ["LBB"])
        ts(NOML[:], LBv[:], -1.0, None, ALU.add, None, ["LBv"], ["NOML"])
        ts(OML[:], LBv[:], -1.0, 1.0, ALU.mult, ALU.add, ["LBv"], ["OML"])
        TILES = [(i * 512, 512) for i in range(4)]

        def rstd_from(ps_ap, n, scale, pk):
            rs, rk = RS.next()
            act(rs[:, 0:n], ps_ap, AF.Ln, [pk], [rk], scale=scale, bias=EPS)
            act(rs[:, 0:n], rs[:, 0:n], AF.Exp, [rk], [rk], scale=-0.5)
            return rs, rk

        def norm_stats(ti):
            c0, n = TILES[ti]
            pb, pk = bank()
            for k in range(8):
                sq, sk = RB.next()
                act(sq[:], xT[:, k, c0:c0 + n], AF.Square, [("x", k, ti)], [sk])
                S.op("pe", MM(pb[:], onesb[:], sq[:], k == 0, k == 7), reads=[sk, "onesb"], writes=[pk])
            act(RSN[:], pb[:], AF.Ln, [pk], ["RSN"], scale=1.0 / D, bias=EPS)
            act(RSN[:], RSN[:], AF.Exp, ["RSN"], ["RSN"], scale=-0.5)

        def norm_apply(ti, dst, dst_keys, a0, s0):
            c0, n = TILES[ti]
            for k in range(8):
                t1, tk = RF.next()
                stt(t1[:], xT[:, k, c0:c0 + n], modT[:, a0 + k, 0:1], RSN[:], ALU.mult, ALU.mult, [("x", k, ti), MK, "RSN"], [tk])
                act(dst(k), t1[:], AF.Identity, [tk, MK], [dst_keys[k]], bias=modT[:, s0 + k, 0:1], scale=1.0)

        def norm_prompt(ti, dst, dst_keys, a0, s0):
            c0, n = TILES[ti]
            pb, pk = bank()
            for k in range(8):
                act(dst(k), xT[:, k, c0:c0 + n], AF.Square, [("x", k, ti)], [dst_keys[k]])
                S.op("pe", MM(pb[:], onesb[:], dst(k), k == 0, k == 7), reads=[dst_keys[k], "onesb"], writes=[pk])
            rs, rk = rstd_from(pb[:], n, 1.0 / D, pk)
            for k in range(8):
                t1, tk = RF.next()
                stt(t1[:], xT[:, k, c0:c0 + n], modT[:, a0 + k, 0:1], rs[:], ALU.mult, ALU.mult, [("x", k, ti), MK, rk], [tk])
                act(dst(k), t1[:], AF.Identity, [tk, MK], [dst_keys[k]], bias=modT[:, s0 + k, 0:1], scale=1.0)

        def norm_sample(dst, dst_key, a0, s0, final=False):
            sq, sk = RB.next()
            sqv = sq[:, 0:8 * NS].rearrange("p (k t) -> p k t", t=NS)
            act(sqv, xT[:, :, NPT:NT], AF.Square, xkeys(4), [sk])
            pb, pk = bank()
            mm_group(pb[:, 0:NS], [(onesb[:], sqv[:, k, :]) for k in range(8)], [sk, "onesb"], [pk])
            rs, rk = rstd_from(pb[:, 0:NS], NS, 1.0 / D, pk)
            tt(tmps[:], xT[:, :, NPT:NT], rs[:, 0:NS].unsqueeze(1).to_broadcast([128, 8, NS]), ALU.mult, xkeys(4) + [rk], ["tmps"])
            if final:
                tt(dst, tmps[:], FG[:].unsqueeze(2).to_broadcast([128, 8, NS]), ALU.mult, ["tmps", "FG"], [dst_key])
            else:
                tt(tmps[:], tmps[:], modT[:, a0:a0 + 8, 1:17], ALU.mult, ["tmps", MK], ["tmps"])
                tt(dst, tmps[:], modT[:, s0:s0 + 8, 1:17], ALU.add, ["tmps", MK], [dst_key])

        def ada_blocks(lx):
            dstT, dk = modTs[lx % 2], ("modT", lx % 2)
            bada, bk_ = BADAs[lx % 2], ("BADA", lx % 2)
            small_T(b_ada[lx].rearrange("(c p) -> c p", p=128), 48, bada[:], bk_, "bada")
            for j in range(24):
                (wv,), wk = wload([w_ada[lx][:, j * 256:(j + 1) * 256]])
                pb, pk = bank()
                for c4 in range(2):
                    mm_group(pb[:, c4 * 17:(c4 + 1) * 17], [(wv[:, k, c4 * 128:(c4 + 1) * 128], condT[:, k, :]) for k in range(8)], wk + ["condT"], [pk])
                tt(dstT[:, 2 * j:2 * j + 2, :], pb[:, 0:34].rearrange("p (c t) -> p c t", t=17),
                   bada[:, 2 * j:2 * j + 2].unsqueeze(2).to_broadcast([128, 2, 17]), ALU.add, [pk, bk_], [(dk, j)])
                if j == 23:
                    allk = [(dk, jj) for jj in range(24)]
                    ts(dstT[:, 8:16, :], dstT[:, 8:16, :], 1.0, None, ALU.add, None, allk, [dk])
                    ts(dstT[:, 32:40, :], dstT[:, 32:40, :], 1.0, None, ALU.add, None, [dk], [dk])
                    if debug:
                        S.dma("sp", dbg["m"][lx], dstT[:], reads=[dk])
                yield j

        def layer_params(l):
            for g in range(4):
                S.dma("sp", bsT[(g % 2) * 64:(g % 2 + 1) * 64, g // 2, :], c_bs[l, g:g + 1, :].partition_broadcast(64), writes=[("bsT", g)])
                S.dma("sp", w00[(g % 2) * 64:(g % 2 + 1) * 64, g // 2:g // 2 + 1], c_ws[l, g, 0:1, 0:1].partition_broadcast(64), writes=[("w00", g)])
                S.dma("sp", b00[(g % 2) * 64:(g % 2 + 1) * 64, g // 2:g // 2 + 1], c_bs[l, g:g + 1, 0:1].partition_broadcast(64), writes=[("b00", g)])
                S.dma("pool", BD[(g % 2) * 64:(g % 2 + 1) * 64, g // 2, (g % 2) * 64:(g % 2 + 1) * 64], pool_map[l, g], reads=["BD"], writes=[("BDg", g)], slot=("bd", g))
            S.dma("sp", lng[:], c_ln_g[l:l + 1, :].partition_broadcast(128), writes=["lng"])
            S.dma("sp", lnb[:], c_ln_b[l:l + 1, :].partition_broadcast(128), writes=["lnb"])
            S.dma("sp", smallin[:], c_ws[l].rearrange("g t s -> t g s"), writes=["PBt"], slot="PBt")
            pb, pk = bank()
            for g in range(4):
                pe_T(pb[:, g * 128:(g + 1) * 128], smallin[:, g * 128:(g + 1) * 128], ident[:], ["PBt", "ident"], [pk])
            S.op("dve", SEL(WmT[:], trimask[:], pb[:].rearrange("p (g t) -> p g t", t=128), zeros[:].unsqueeze(1).to_broadcast([128, 4, 128])),
                 reads=[pk, "trimask", "zeros"], writes=["WmT"])
            memset(Sst[:], 0.0, ["Sst"])
            memset(Sb[1][:], 0.0, [("Sb", 1)])

        BSK = [("bsT", g) for g in range(4)]
        W0K = [("w00", g) for g in range(4)] + [("b00", g) for g in range(4)]
        BDK = ["BD"] + [("BDg", g) for g in range(4)]

        def tm_group(l, gi):
            c0, n = TILES[gi]
            smp = (gi == 3)
            HK = [("h", k) for k in range(8)]
            Wl = w_in[l]
            if gi == 0:
                norm_stats(0)
                norm_apply(0, lambda k: hTg[:, k, :], HK, 8, 0)
            if smp:
                norm_sample(hTs[:], "hTs", 8, 0)

            for jb in range(2):
                (wv,), wk = wload([Wl[:, 1024 + jb * 256:1024 + (jb + 1) * 256]])
                for s4 in range(4):
                    pb, pk = bank()
                    mm_group(pb[:, 0:256], [(hTg[:, k, s4 * 128:(s4 + 1) * 128], wv[:, k, :]) for k in range(8)], wk + HK, [pk])
                    cp(Vtok[:, s4, jb * 256:(jb + 1) * 256], pb[:, 0:256], [pk], [("Vtok", s4, jb)], eng=("act" if s4 % 2 == 0 else "dve"))
                if smp:
                    pb, pk = bank()
                    mm_group(pb[0:NS, 0:256], [(hTs[:, k, :], wv[:, k, :]) for k in range(8)], wk + ["hTs"], [pk])
                    cp(Vs[:, jb * 256:(jb + 1) * 256], pb[0:NS, 0:256], [pk], [("Vs", jb)])
            VK = lambda s4: [("Vtok", s4, 0), ("Vtok", s4, 1)]

            (wvp,), wkp = wload([Wl[:, 2048:2304]])
            if gi == 0:
                memset(PBt[:, :, 0:15], 0.0, ["PBt"])
            else:
                for c2 in range(2):
                    cp(PA[:, 0:15], PBt[:, c2, 512:527], ["PBt"], ["PA"])
                    cp(PBt[:, c2, 0:15], PA[:, 0:15], ["PA"], ["PBt"])
            for c2 in range(2):
                pb, pk = bank()
                mm_group(pb[:], [(wvp[:, k, c2 * 128:(c2 + 1) * 128], hTg[:, k, :]) for k in range(8)], wkp + HK, [pk])
                cp(PBt[:, c2, 15:527], pb[:], [pk], ["PBt"], eng="act")
            if smp:
                pb, pk = bank()
                mm_group(pb[0:NS, 0:256], [(hTg[:, k, 496:512], wvp[:, k, :]) for k in range(8)], wkp + HK, [pk])
                cp(tail[:], pb[0:NS, 0:256], [pk], ["tail"])
                S.dma("sp", pl_p[l], tail[1:16, :], reads=["tail"], slot="plp")
                S.dma("sp", Xa[0:120, :], spl[l, 0:8].rearrange("b r c -> (b r) c"), writes=["Xa"], slot="Xa")
                S.dma("sp", Xb[0:120, :], spl[l, 8:16].rearrange("b r c -> (b r) c"), writes=["Xb"], slot="Xb")
                S.dma("sp", pl_s[l, :, 0:14, :], spl[l, :, 1:15, :], reads=[("dramcopy", l)], slot="plsd")
                pb, pk = bank()
                mm_group(pb[0:NS, 0:256], [(hTs[:, k, :], wvp[:, k, :]) for k in range(8)], wkp + ["hTs"], [pk])
                cp(pnew[:], pb[0:NS, 0:256], [pk], ["pnew"])
                S.dma("sp", pl_s[l, :, 14, :], pnew[:], reads=["pnew"], slot="pls")
                pb, pk = bank()
                for c2 in range(2):
                    mm_group(pb[:, c2 * NS:(c2 + 1) * NS], [(wvp[:, k, c2 * 128:(c2 + 1) * 128], hTs[:, k, :]) for k in range(8)], wkp + ["hTs"], [pk])
                cp(xnf[:], pb[:, 0:2 * NS].rearrange("p (c t) -> p c t", t=NS), [pk], ["xnf"])
                pz, pzk = bank()
                fns = []
                for c2 in range(2):
                    for gh in range(2):
                        g = 2 * c2 + gh
                        o_ = pz[gh * 64:(gh + 1) * 64, c2 * NS:(c2 + 1) * NS]
                        fns.append(MM(o_, Xa[0:120, g * 64:(g + 1) * 64], selw[0:120, (g * 2) * 16:(g * 2 + 1) * 16], True, False))
                        fns.append(MM(o_, Xb[0:120, g * 64:(g + 1) * 64], selw[0:120, (g * 2 + 1) * 16:(g * 2 + 2) * 16], False, True))
                S.op("pe", fns, reads=["Xa", "Xb", "selw"], writes=[pzk])
                for c2 in range(2):
                    t1, tk = RF.next()
                    tt(t1[:, 0:NS], pz[:, c2 * NS:(c2 + 1) * NS], xnf[:, c2, :], ALU.add, [pzk, "xnf"], [tk])
                    stt(ZBs[:, c2, :], t1[:, 0:NS], invw[:, c2:c2 + 1], xnf[:, c2, :], ALU.mult, ALU.subtract, [tk, "invw", "xnf"], ["ZBs"])

            def zb_from(src, c2, lo, hi, key):
                stt(ZB[lo:hi, c2, :], src[lo:hi, 15:527], invw[lo:hi, c2:c2 + 1], PBt[lo:hi, c2, 15:527], ALU.mult, ALU.subtract,
                    [key, "invw", "PBt"], [("ZB", c2, lo)])
                if gi == 0:
                    t1, tk = RF.next()
                    tt(t1[lo:hi, 0:16], src[lo:hi, 15:31], rc16[lo:hi, c2, :], ALU.mult, [key, "rc16"], [tk])
                    tt(ZB[lo:hi, c2, 0:16], t1[lo:hi, 0:16], PBt[lo:hi, c2, 15:31], ALU.subtract, [tk, "PBt", ("ZB", c2, lo)], [("ZB", c2, lo)])
            tt(PA[:, 1:527], PBt[:, 0, 1:527], PBt[:, 0, 0:526], ALU.add, ["PBt"], ["PA"])
            zb_from(PA, 0, 0, 64, "PA")
            tt(PBb[:, 3:527], PA[:, 3:527], PA[:, 1:525], ALU.add, ["PA"], ["PBb"])
            zb_from(PBb, 0, 64, 128, "PBb")
            tt(PA[:, 1:527], PBt[:, 1, 1:527], PBt[:, 1, 0:526], ALU.add, ["PBt"], ["PA"])
            tt(PBb[:, 3:527], PA[:, 3:527], PA[:, 1:525], ALU.add, ["PA"], ["PBb"])
            tt(PA[:, 7:527], PBb[:, 7:527], PBb[:, 3:523], ALU.add, ["PBb"], ["PA"])
            zb_from(PA, 1, 0, 64, "PA")
            tt(PBb[:, 15:527], PA[:, 15:527], PA[:, 7:519], ALU.add, ["PA"], ["PBb"])
            zb_from(PBb, 1, 64, 128, "PBb")
            ZBK = [("ZB", c2, lo) for c2 in range(2) for lo in (0, 64)]

            (wv,), wk = wload([Wl[:, 2560:2816]])
            gl = []
            for s4 in range(4):
                pb, pk = bank()
                mm_group(pb[:, 0:256], [(hTg[:, k, s4 * 128:(s4 + 1) * 128], wv[:, k, :]) for k in range(8)], wk + HK, [pk])
                g, gk = RF.next()
                act(g[:, 0:256], pb[:, 0:256], AF.Gelu_apprx_tanh, [pk], [gk])
                gl.append((g[:, 0:256], gk, 128))
            if smp:
                pb, pk = bank()
                mm_group(pb[0:NS, 0:256], [(hTs[:, k, :], wv[:, k, :]) for k in range(8)], wk + ["hTs"], [pk])
                act(vns[:], pb[0:NS, 0:256], AF.Gelu_apprx_tanh, [pk], ["vns"])
                gl.append((vns[:], "vns", NS))
            memset(stat[:, 6:16], 1.0, ["stat"])
            for i, (g, gk, np_) in enumerate(gl):
                S.op("dve", lambda e, g=g, np_=np_: e.bn_stats(out=stat[0:np_, 0:6], in_=g[0:np_, :]), reads=[gk, "stat"], writes=["stat"])
                S.op("dve", lambda e, i=i, np_=np_: e.bn_aggr(out=stat[0:np_, 6 + 2 * i:8 + 2 * i], in_=stat[0:np_, 0:6]), reads=["stat"], writes=["stat"])
            var5 = stat[:, 6:16].rearrange("p (i t) -> p i t", t=2)[:, :, 1]
            rstd5 = stat[:, 16:21]
            act(rstd5, var5, AF.Ln, ["stat"], ["stat"], scale=1.0, bias=EPS)
            act(rstd5, rstd5, AF.Exp, ["stat"], ["stat"], scale=-0.5)
            for i, (g, gk, np_) in enumerate(gl):
                ts(g[0:np_, :], g[0:np_, :], stat[0:np_, 6 + 2 * i:7 + 2 * i], stat[0:np_, 16 + i:17 + i], ALU.subtract, ALU.mult, [gk, "stat"], [gk])
                tt(g[0:np_, :], g[0:np_, :], lng[0:np_, :], ALU.mult, [gk, "lng"], [gk])
                if i < 4:
                    tt(VN[:, i, :], g[0:np_, :], lnb[0:np_, :], ALU.add, [gk, "lnb"], [("VN", i)])
                else:
                    tt(vns[:], vns[:], lnb[0:NS, :], ALU.add, ["vns", "lnb"], ["vns"])
                    S.dma("sp", cv_s[l], vns[:], reads=["vns"], slot="cv")

            for jb in range(2):
                (wv,), wk = wload([Wl[:, 1536 + jb * 256:1536 + (jb + 1) * 256]])
                for hh in range(2):
                    h = jb * 2 + hh
                    pb, pk = bank()
                    mm_group(pb[:], [(wv[:, k, hh * 128:(hh + 1) * 128], hTg[:, k, :]) for k in range(8)], wk + HK, [pk])
                    t1, tk = RF.next()
                    act(t1[:], pb[:], AF.Silu, [pk], [tk])
                    ts(GS[:, h, :], t1[:], ANG[:, l * 4 + h:l * 4 + h + 1], None, ALU.mult, None, [tk, "ANG"], [("GS", h)])
                if smp:
                    pb, pk = bank()
                    for hh in range(2):
                        mm_group(pb[:, hh * NS:(hh + 1) * NS], [(wv[:, k, hh * 128:(hh + 1) * 128], hTs[:, k, :]) for k in range(8)], wk + ["hTs"], [pk])
                    act(GSs[:, jb * 2:jb * 2 + 2, :], pb[:, 0:2 * NS].rearrange("p (h t) -> p h t", t=NS), AF.Silu, [pk], [("GSs", jb)])
                    for hh in range(2):
                        h = jb * 2 + hh
                        ts(GSs[:, h, :], GSs[:, h, :], ANG[:, l * 4 + h:l * 4 + h + 1], None, ALU.mult, None, [("GSs", jb), "ANG"], [("GSs", jb)])
            GSsK = [("GSs", 0), ("GSs", 1)]

            oml = lambda h: OML[:, h, l:l + 1]
            noml = lambda h: NOML[:, h, l:l + 1]
            lbb = lambda h: LBB[:, h, l:l + 1]
            LK = ["OML", "NOML", "LBB"]
            qk, kkk, bk = ("QS", 0), ("KK", 0), ("BB", 0)
            qs, kk, b_ = QS[0], KK[0], BB[0]
            wqf = {}

            def qf_mm(h):
                jb, hh = h // 2, h % 2
                if hh == 0:
                    (wq,), wkq = wload([Wl[:, jb * 256:(jb + 1) * 256]])
                    (wf,), wkf = wload([Wl[:, 512 + jb * 256:512 + (jb + 1) * 256]])
                    wqf[jb] = (wq, wkq, wf, wkf)
                wq, wkq, wf, wkf = wqf[jb]
                hc = slice(hh * 128, (hh + 1) * 128)
                pq, pqk = bank()
                mm_group(pq[:], [(wq[:, k, hc], hTg[:, k, :]) for k in range(8)], wkq + HK, [pqk])
                pf, pfk = bank()
                mm_group(pf[:], [(wf[:, k, hc], hTg[:, k, :]) for k in range(8)], wkf + HK, [pfk])
                ps_ = None
                if smp:
                    ps_, psk = bank()
                    mm_group(ps_[:, 0:NS], [(wq[:, k, hc], hTs[:, k, :]) for k in range(8)], wkq + ["hTs"], [psk])
                    mm_group(ps_[:, NS:2 * NS], [(wf[:, k, hc], hTs[:, k, :]) for k in range(8)], wkf + ["hTs"], [psk])
                    ps_ = (ps_, psk)
                return pq, pqk, pf, pfk, ps_

            def sigmoid_lnexp(dst, src, n, rk, wk_):
                act(dst, src, AF.Exp, rk, [wk_], scale=-1.0)
                act(dst, dst, AF.Ln, [wk_], [wk_], bias=1.0, scale=1.0)
                act(dst, dst, AF.Exp, [wk_], [wk_], scale=-1.0)

            (wv,), wk = wload([Wl[:, 2304:2560]])
            for c2 in range(2):
                pb, pk = bank()
                mm_group(pb[:], [(wv[:, k, c2 * 128:(c2 + 1) * 128], hTg[:, k, :]) for k in range(8)], wk + HK, [pk])
                pb2, pk2 = bank()
                fns = []
                for s4 in range(4):
                    for gh in range(2):
                        g = 2 * c2 + gh
                        fns.append(MM(pb2[gh * 64:(gh + 1) * 64, s4 * 128:(s4 + 1) * 128], VN[:, s4, g * 64:(g + 1) * 64], WmT[:, g, :]))
                S.op("pe", fns, reads=[("VN", s4) for s4 in range(4)] + ["WmT"], writes=[pk2])
                ug, uk = RF.next()
                act(ug[:], pb[:], AF.Gelu_apprx_tanh, [pk], [uk])
                t1, tk = RF.next()
                tt(t1[:].rearrange("p (s t) -> p s t", t=128), pb2[:].rearrange("p (s t) -> p s t", t=128),
                   bsT[:, c2, :].unsqueeze(1).to_broadcast([128, 4, 128]), ALU.add, [pk2] + BSK, [tk])
                tt(GM[:, c2, :], t1[:], ug[:], ALU.mult, [tk, uk], [("GM", c2)])
            if smp:
                pb, pk = bank()
                for c2 in range(2):
                    mm_group(pb[:, c2 * NS:(c2 + 1) * NS], [(wv[:, k, c2 * 128:(c2 + 1) * 128], hTs[:, k, :]) for k in range(8)], wk + ["hTs"], [pk])
                act(us_s[:], pb[:, 0:2 * NS].rearrange("p (c t) -> p c t", t=NS), AF.Gelu_apprx_tanh, [pk], ["us_s"])
                pb, pk = bank()
                for c2 in range(2):
                    pe_T(pb[:, c2 * NS:(c2 + 1) * NS], vns[:, c2 * 128:(c2 + 1) * 128], ident[0:NS, 0:NS], ["vns", "ident"], [pk])
                for c2 in range(2):
                    t1, tk = RF.next()
                    ts(t1[:, 0:NS], pb[:, c2 * NS:(c2 + 1) * NS], w00[:, c2:c2 + 1], b00[:, c2:c2 + 1], ALU.mult, ALU.add, [pk] + W0K, [tk])
                    tt(GMs[:, c2, :], t1[:, 0:NS], us_s[:, c2, :], ALU.mult, [tk, "us_s"], ["GMs"])

            for c2 in range(2):
                pb, pk = bank()
                mm_group(pb[:], [(BD[:, c2, :], ZB[:, c2, :])], BDK + ZBK, [pk])
                act(PBM[:, c2, :], pb[:], AF.Identity, [pk, "PSC"], [("PBM", c2)], scale=PSC[:, l * 2 + c2:l * 2 + c2 + 1], bias=0.0)
            if smp:
                pb, pk = bank()
                for c2 in range(2):
                    mm_group(pb[:, c2 * NS:(c2 + 1) * NS], [(BD[:, c2, :], ZBs[:, c2, :])], BDK + ["ZBs"], [pk])
                for c2 in range(2):
                    act(PBMs[:, c2, :], pb[:, c2 * NS:(c2 + 1) * NS], AF.Identity, [pk, "PSC"], ["PBMs"], scale=PSC[:, l * 2 + c2:l * 2 + c2 + 1], bias=0.0)

            nxt = qf_mm(0)
            pending = []
            HF = [(0, slice(0, 256)), (1, slice(256, 512))]
            for h in range(4):
                pq, pqk, pf, pfk, ps_ = nxt
                if h + 1 < 4:
                    nxt = qf_mm(h + 1)
                a1, a1k = RF.next()
                sg, sgk = RF.next()
                for hf, cs in HF:
                    act(a1[:, cs], pq[:, cs], AF.Exp, [pqk], [(a1k, hf), a1k], scale=-1.0)
                for hf, cs in HF:
                    act(sg[:, cs], pf[:, cs], AF.Exp, [pfk], [(sgk, hf), sgk], scale=-1.0)
                for t_, tk_ in ((a1, a1k), (sg, sgk)):
                    for hf, cs in HF:
                        act(t_[:, cs], t_[:, cs], AF.Ln, [(tk_, hf)], [(tk_, hf)], bias=1.0, scale=1.0)
                    for hf, cs in HF:
                        act(t_[:, cs], t_[:, cs], AF.Exp, [(tk_, hf)], [(tk_, hf)], scale=-1.0)
                while pending:
                    pending.pop()()
                lf, lfk = RF.next()
                for hf, cs in HF:
                    tt(qs[:, cs], pq[:, cs], a1[:, cs], ALU.mult, [pqk, (a1k, hf)], [(qk, hf)])
                    act(lf[:, cs], sg[:, cs], AF.Ln, [(sgk, hf)] + LK, [(lfk, hf), lfk], scale=oml(h), bias=lbb(h))
                for hf, cs in HF:
                    ts(kk[:, cs], sg[:, cs], noml(h), oml(h), ALU.mult, ALU.add, [(sgk, hf)] + LK, [(kkk, hf)])
                    S.op("dve", lambda e, lf=lf, cs=cs: e.tensor_tensor_scan(out=b_[:, cs], data0=rmask[:, cs], data1=lf[:, cs], initial=0.0,
                                                                             op0=ALU.mult, op1=ALU.add),
                         reads=["rmask", (lfk, hf)], writes=[(bk, hf)])
                d1, d1k = RF.next()
                d3, d3k = RF.next()
                b3 = lambda t, cs: t[:, cs].rearrange("p (c t) -> p c t", t=64)
                for hf, cs in HF:
                    act(EB[:, h, 4 * hf:4 * hf + 4], b3(b_, cs)[:, :, 63], AF.Exp, [(bk, hf)], [("EB", h, hf)])
                    tt(b3(d1, cs), b3(b_, cs), b3(b_, cs)[:, :, 31:32].to_broadcast([128, 4, 64]), ALU.subtract, [(bk, hf)], [(d1k, hf), d1k])
                    tt(b3(d3, cs), b3(b_, cs)[:, :, 63:64].to_broadcast([128, 4, 64]), b3(b_, cs), ALU.subtract, [(bk, hf)], [(d3k, hf), d3k])
                e0, e0k = RF.next()
                e1, e1k = RF.next()
                for hf, cs in HF:
                    act(e0[:, cs], b_[:, cs], AF.Exp, [(bk, hf)], [(e0k, hf), e0k])
                    act(e1[:, cs], d1[:, cs], AF.Exp, [(d1k, hf)], [(e1k, hf), e1k])
                for hf, cs in HF:
                    act(d1[:, cs], d1[:, cs], AF.Exp, [(d1k, hf)], [(d1k, hf)], scale=-1.0)
                    act(d3[:, cs], d3[:, cs], AF.Exp, [(d3k, hf)], [(d3k, hf)])
                for hf, cs in HF:
                    tt(QH[:, h, cs], qs[:, cs], e0[:, cs], ALU.mult, [(qk, hf), (e0k, hf)], [("QX", h, hf), ("QX", h)])
                    tt(QT[:, h, cs], qs[:, cs], e1[:, cs], ALU.mult, [(qk, hf), (e1k, hf)], [("QX", 4 + h, hf), ("QX", 4 + h)])
                for hf, cs in HF:
                    tt(KT[:, h, cs], kk[:, cs], d1[:, cs], ALU.mult, [(kkk, hf), (d1k, hf)], [("KT", h, hf)])
                    tt(KHTb[:, cs], kk[:, cs], d3[:, cs], ALU.mult, [(kkk, hf), (d3k, hf)], [("KHTb", hf)])
                for s4 in range(4):
                    pe_T(PSB[:, s4 * 128:(s4 + 1) * 128], KHTb[:, s4 * 128:(s4 + 1) * 128], identb[:], [("KHTb", 0), ("KHTb", 1), "identb"], ["psb"])
                pending.append(lambda h=h: cp(KHtok[:, h, :, :], PSB[:, 0:512].rearrange("p (s k) -> p s k", k=128), ["psb"],
                                              [("KHtok", h, 0), ("KHtok", h, 1)], eng="act"))
                if smp:
                    pb, pk = ps_
                    sg2, sg2k = RF.next()
                    sigmoid_lnexp(sg2[:, 0:2 * NS], pb[:, 0:2 * NS], 2 * NS, [pk], sg2k)
                    tt(qs_s[:, h, :], pb[:, 0:NS], sg2[:, 0:NS], ALU.mult, [pk, sg2k], [("qs_s", h)])
                    ts(kk_s[:, h, :], sg2[:, NS:2 * NS], noml(h), oml(h), ALU.mult, ALU.add, [sg2k] + LK, [("kk_s", h)])
                    ts(f_s[:, h, :], sg2[:, NS:2 * NS], oml(h), lbb(h), ALU.mult, ALU.add, [sg2k] + LK, [("f_s", h)])
            while pending:
                pending.pop()()

            QHK = [("QX", h, hf) for h in range(4) for hf in range(2)]
            QTK = [("QX", 4 + h, hf) for h in range(4) for hf in range(2)]
            KTK = [("KT", h, hf) for h in range(4) for hf in range(2)]
            KHK = [("KHtok", h, hf) for h in range(4) for hf in range(2)]
            GSK = [("GS", h) for h in range(4)]
            EBK = [("EB", h, hf) for h in range(4) for hf in range(2)]
            S4v = lambda t: t[:].rearrange("p (h t) -> p h t", t=128)

            def scores(s4):
                cs = slice(s4 * 128, (s4 + 1) * 128)
                pc, pck = bank()
                fns = [MM(pc[:, h * 128:(h + 1) * 128], KT[:, h, cs], QT[:, h, cs]) for h in range(4)]
                S.op("pe", fns, reads=KTK + QTK, writes=[pck])
                sc, sck = scT[s4 % 3], ("scT", s4 % 3)
                S.op("dve", SEL(sc, scmask[:], S4v(pc), zeros[:].unsqueeze(1).to_broadcast([128, 4, 128])), reads=[pck, "scmask", "zeros"], writes=[sck])

            def state_update(c):
                s4, half = c // 2, c % 2
                lo, hi = half * 64, half * 64 + 64
                pd, pdk = bank()
                fns = [MM(pd[:, h * 128:(h + 1) * 128], KHtok[lo:hi, h, s4, :], Vtok[lo:hi, s4, h * 128:(h + 1) * 128]) for h in range(4)]
                S.op("pe", fns, reads=KHK + VK(s4), writes=[pdk])
                tt(Sst[:], Sst[:], EB[:, :, c:c + 1].to_broadcast([128, 4, 128]), ALU.mult, ["Sst"] + EBK, ["Sst"])
                tt(Sst[:], Sst[:], S4v(pd), ALU.add, ["Sst", pdk], ["Sst"])
                cp(Sb[c % 2][:], Sst[:], ["Sst"], [("Sb", c % 2)], eng="act")

            def sample_hgrn():
                SK = [("qs_s", h) for h in range(4)]
                po, pok = PSL, "psl"

                def bcast(b):
                    s0, s0k = sS0[b % 2], ("sS0", b % 2)
                    S.dma("sp", s0[:], sh[l, b].rearrange("h k v -> k h v"), writes=[s0k], slot=s0k)
                    pv, pvk = bank()
                    S.op("pe", MM(pv[:], ident[0:NS, b:b + 1].to_broadcast([NS, 128]), Vs[:]), reads=["ident", ("Vs", 0), ("Vs", 1)], writes=[pvk])
                    return pv, pvk
                nxtv = bcast(0)
                for b in range(NS):
                    pv, pvk = nxtv
                    s0, s0k = sS0[b % 2], ("sS0", b % 2)
                    sn, snk = s0, s0k
                    for h in range(4):
                        act(s0[:, h, :], s0[:, h, :], AF.Identity, [s0k, ("f_s", h)], [s0k], scale=f_s[:, h, b:b + 1], bias=0.0)
                    for h in range(4):
                        stt(s0[:, h, :], pv[:, h * 128:(h + 1) * 128], kk_s[:, h, b:b + 1], s0[:, h, :], ALU.mult, ALU.add,
                            [pvk, ("kk_s", h), s0k], [s0k])
                    if b + 1 < NS:
                        nxtv = bcast(b + 1)
                    S.dma("sp", hg_s[l, b].rearrange("h k v -> k h v"), sn[:], reads=[snk], slot=("hgs", b % 2))
                    fns = [MM(po[:, h * NS + b:h * NS + b + 1], sn[:, h, :], qs_s[:, h, b:b + 1]) for h in range(4)]
                    S.op("pe", fns, reads=[snk] + SK, writes=[pok])
                    yield b
                sq, sqk = RB.next()
                act(sq[:, 0:4 * NS], po[:, 0:4 * NS], AF.Square, [pok], [sqk])
                pn, pnk = bank()
                mm_group(pn[:, 0:4 * NS], [(onesb[:], sq[:, 0:4 * NS])], [sqk, "onesb"], [pnk])
                rs, rk = RS.next()
                act(rs[:, 0:4 * NS], pn[:, 0:4 * NS], AF.Ln, [pnk], [rk], scale=1.0 / 128, bias=EPS)
                act(rs[:, 0:4 * NS], rs[:, 0:4 * NS], AF.Exp, [rk], [rk], scale=-0.5)
                t1, tk = RF.next()
                tt(t1[:, 0:4 * NS], po[:, 0:4 * NS], rs[:, 0:4 * NS], ALU.mult, [pok, rk], [tk])
                tt(OTs[:], t1[:, 0:4 * NS].rearrange("p (h t) -> p h t", t=NS), GSs[:], ALU.mult, [tk] + GSsK, ["OTs"])

            smp_it = sample_hgrn() if smp else iter(())
            scores(0)
            scores(1)
            state_update(0)
            for s4 in range(4):
                cs = slice(s4 * 128, (s4 + 1) * 128)
                sc, sck = scT[s4 % 3], ("scT", s4 % 3)
                po, pok = bank()
                fns = []
                for h in range(4):
                    fns.append(MM(po[:, h * 128:(h + 1) * 128], Vtok[:, s4, h * 128:(h + 1) * 128], sc[:, h, :], True, False, True))
                    fns.append(MM(po[:, h * 128:h * 128 + 64], Sb[1][:, h, :], QH[:, h, s4 * 128:s4 * 128 + 64], False, False, True))
                    fns.append(MM(po[:, h * 128 + 64:(h + 1) * 128], Sb[0][:, h, :], QH[:, h, s4 * 128 + 64:(s4 + 1) * 128], False, True, True))
                S.op("pe", fns, reads=VK(s4) + [sck, ("Sb", 0), ("Sb", 1)] + QHK, writes=[pok])
                state_update(2 * s4 + 1)
                if s4 + 1 < 4:
                    state_update(2 * s4 + 2)
                if s4 + 2 < 4:
                    scores(s4 + 2)
                sq, sqk = RB.next()
                act(sq[:], po[:], AF.Square, [pok], [sqk])
                pn, pnk = bank()
                mm_group(pn[:], [(onesb[:], sq[:])], [sqk, "onesb"], [pnk])
                rs, rk = RS.next()
                act(rs[:], pn[:], AF.Ln, [pnk], [rk], scale=1.0 / 128, bias=EPS)
                act(rs[:], rs[:], AF.Exp, [rk], [rk], scale=-0.5)
                t1, tk = RF.next()
                tt(t1[:], po[:], rs[:], ALU.mult, [pok, rk], [tk])
                tt(OT[:, :, cs], S4v(t1), GS[:, :, cs], ALU.mult, [tk] + GSK, [("OT", s4)])
                for _ in range(4):
                    next(smp_it, None)
            OTK = [("OT", s4) for s4 in range(4)]
            if gi == 3:
                S.dma("sp", hg_p[l].rearrange("h k v -> k h v"), Sst[:], reads=["Sst"], slot="hgp")
            for _ in smp_it:
                pass

            if gi + 1 < 4:
                norm_stats(gi + 1)

            PBMK = [("PBM", c2) for c2 in range(2)]
            GMK = [("GM", c2) for c2 in range(2)]
            for d in range(8):
                dc = slice(d * 128, (d + 1) * 128)
                specs = [(2816, w_br_a, OT, 4, OTK, OTs, "OTs"), (3840, w_br_b, PBM, 2, PBMK, PBMs, "PBMs"), (4864, w_br_c, GM, 2, GMK, GMs, "GMs")]
                for pi, (gc0, wbr, src, nk, srck, srcs, srcsk) in enumerate(specs):
                    (wg, wy), wk = wload([Wl[:, gc0 + d * 128:gc0 + (d + 1) * 128], wbr[l][:, dc]])
                    pg, pgk = bank()
                    mm_group(pg[:], [(wg[:, k, :], hTg[:, k, :]) for k in range(8)], wk + HK, [pgk])
                    py, pyk = bank()
                    mm_group(py[:], [(wy[:, j, :], src[:, j, :]) for j in range(nk)], wk + srck, [pyk])
                    sg, sgk = RF.next()
                    act(sg[:], pg[:], AF.Sigmoid, [pgk], [sgk])
                    if pi == 0:
                        tt(MACC[:], sg[:], py[:], ALU.mult, [sgk, pyk], ["MACC"])
                    else:
                        t1, tk = RF.next()
                        tt(t1[:], sg[:], py[:], ALU.mult, [sgk, pyk], [tk])
                        if pi == 1:
                            tt(MACC[:], MACC[:], t1[:], ALU.add, ["MACC", tk], ["MACC"])
                        else:
                            tt(MT[:, d, :], MACC[:], t1[:], ALU.add, ["MACC", tk], [("QX", d), ("QX", d, 0), ("QX", d, 1)])
                    if smp:
                        pb, pk = bank()
                        mm_group(pb[:, 0:NS], [(wg[:, k, :], hTs[:, k, :]) for k in range(8)], wk + ["hTs"], [pk])
                        mm_group(pb[:, NS:2 * NS], [(wy[:, j, :], srcs[:, j, :]) for j in range(nk)], wk + [srcsk], [pk])
                        sg, sgk = RF.next()
                        act(sg[:, 0:NS], pb[:, 0:NS], AF.Sigmoid, [pk], [sgk])
                        if pi == 0:
                            tt(maccs[:], sg[:, 0:NS], pb[:, NS:2 * NS], ALU.mult, [sgk, pk], ["maccs"])
                        else:
                            tt(sg[:, 0:NS], sg[:, 0:NS], pb[:, NS:2 * NS], ALU.mult, [sgk, pk], [sgk])
                            if pi == 1:
                                tt(maccs[:], maccs[:], sg[:, 0:NS], ALU.add, ["maccs", sgk], ["maccs"])
                            else:
                                tt(MTs[:, d, :], maccs[:], sg[:, 0:NS], ALU.add, ["maccs", sgk], ["MTs"])

            if gi + 1 < 4:
                norm_apply(gi + 1, lambda k: hTg[:, k, :], HK, 8, 0)

            MTK = [("QX", d) for d in range(8)]
            for jb in range(4):
                (wv,), wk = wload([w_out[l][:, jb * 256:(jb + 1) * 256]])
                for d2 in range(2):
                    dd = jb * 2 + d2
                    pb, pk = bank()
                    mm_group(pb[:], [(wv[:, k, d2 * 128:(d2 + 1) * 128], MT[:, k, :]) for k in range(8)], wk + MTK, [pk])
                    stt(xT[:, dd, c0:c0 + n], pb[:], modT[:, 16 + dd, 0:1], xT[:, dd, c0:c0 + n], ALU.mult, ALU.add, [pk, MK, ("x", dd, gi)], [("x", dd, gi)])
                    if smp:
                        pb, pk = bank()
                        mm_group(pb[:, 0:NS], [(wv[:, k, d2 * 128:(d2 + 1) * 128], MTs[:, k, :]) for k in range(8)], wk + ["MTs"], [pk])
                        t1, tk = RF.next()
                        tt(t1[:, 0:NS], pb[:, 0:NS], modT[:, 16 + dd, 1:17], ALU.mult, [pk, MK], [tk])
                        tt(xT[:, dd, NPT:NT], xT[:, dd, NPT:NT], t1[:, 0:NS], ALU.add, [tk, ("x", dd, 4)], [("x", dd, 4)])

        ALLT = TILES + [(NPT, NS)]

        def ffn(l):
            for ti in range(4):
                c0, n = TILES[ti]
                norm_prompt(ti, lambda k, c0=c0, n=n: h2T[:, k, c0:c0 + n], [("h2", k, ti) for k in range(8)], 32, 24)
            norm_sample(h2T[:, :, NPT:NT], ("h2s",), 32, 24)
            h2k = lambda ti: ([("h2", k, ti) for k in range(8)] if ti < 4 else [("h2s",)])
            Wi = w_ffn_in[l]
            Wo = w_ffn_out[l]
            ada_it = ada_blocks(l + 1) if l + 1 < nl else iter(())
            for (j0, nj) in FFN_ROUNDS:
                for jj in range(nj):
                    j = j0 + jj
                    next(ada_it, None)
                    (wg, wu), wk = wload([Wi[:, j * 128:(j + 1) * 128], Wi[:, DFF + j * 128:DFF + (j + 1) * 128]])
                    for ti, (c0, n) in enumerate(ALLT):
                        pg, pgk = bank()
                        mm_group(pg[:, 0:n], [(wg[:, k, :], h2T[:, k, c0:c0 + n]) for k in range(8)], wk + h2k(ti), [pgk])
                        pu, puk = bank()
                        mm_group(pu[:, 0:n], [(wu[:, k, :], h2T[:, k, c0:c0 + n]) for k in range(8)], wk + h2k(ti), [puk])
                        sl, slk = RF.next()
                        act(sl[:, 0:n], pg[:, 0:n], AF.Silu, [pgk], [slk])
                        tt(ACTB[:, jj, c0:c0 + n], sl[:, 0:n], pu[:, 0:n], ALU.mult, [slk, puk], [("actb", jj, ti)])
                for dp in range(4):
                    if dp == 0:
                        next(ada_it, None)
                    views, wk = wload([Wo[j0 * 128:(j0 + nj) * 128, (2 * dp + q) * 128:(2 * dp + q + 1) * 128] for q in range(2)])
                    for q in range(2):
                        dd = 2 * dp + q
                        wv = views[q]
                        for ti, (c0, n) in enumerate(ALLT):
                            pb, pk = bank()
                            mm_group(pb[:, 0:n], [(wv[:, j, :], ACTB[:, j, c0:c0 + n]) for j in range(nj)], wk + [("actb", j, ti) for j in range(nj)], [pk])
                            if ti < 4:
                                stt(xT[:, dd, c0:c0 + n], pb[:, 0:n], modT[:, 40 + dd, 0:1], xT[:, dd, c0:c0 + n], ALU.mult, ALU.add,
                                    [pk, MK, ("x", dd, ti)], [("x", dd, ti)])
                            else:
                                t1, tk = RF.next()
                                tt(t1[:, 0:NS], pb[:, 0:NS], modT[:, 40 + dd, 1:17], ALU.mult, [pk, MK], [tk])
                                tt(xT[:, dd, NPT:NT], xT[:, dd, NPT:NT], t1[:, 0:NS], ALU.add, [tk, ("x", dd, 4)], [("x", dd, 4)])

        ada_it0 = ada_blocks(0)
        for i in range(NPT // 128):
            next(ada_it0, None)
            xi = xin[i % 2]
            xk = ("xin", i % 2)
            S.dma("sp", xi[:], xp[i * 128:(i + 1) * 128, :], writes=[xk], slot=xk)
            ti = i // 4
            for half in range(2):
                pb, pk = bank()
                for j in range(4):
                    k = half * 4 + j
                    pe_T(pb[:, j * 128:(j + 1) * 128], xi[:, k * 128:(k + 1) * 128], ident[:], [xk, "ident"], [pk])
                cp(xT[:, half * 4:half * 4 + 4, i * 128:(i + 1) * 128], pb[:].rearrange("p (j t) -> p j t", t=128),
                   [pk], [("x", half * 4 + j, ti) for j in range(4)], eng=("act" if half else "dve"))
        S.dma("sp", xin[0][0:NS, :], xs, writes=[("xin", 0)], slot=("xin", 0))
        pb, pk = bank()
        for k in range(8):
            pe_T(pb[:, k * NS:(k + 1) * NS], xin[0][0:NS, k * 128:(k + 1) * 128], ident[0:NS, 0:NS], [("xin", 0), "ident"], [pk])
        cp(xT[:, :, NPT:NT], pb[:, 0:8 * NS].rearrange("p (k t) -> p k t", t=NS), [pk], xkeys(4))
        for _ in ada_it0:
            pass
        if debug:
            S.dma("sp", dbg["x"][0], xT[:], reads=[k for ti in range(5) for k in xkeys(ti)])

        S.barrier()
        for l in range(nl):
            modT, MK = modTs[l % 2], ("modT", l % 2)
            layer_params(l)
            for gi in range(4):
                tm_group(l, gi)
            S.barrier()
            ffn(l)
            S.barrier()
            if debug:
                S.dma("sp", dbg["x"][l + 1], xT[:], reads=[k for ti in range(5) for k in xkeys(ti)])

        YT = AFt[:, 0:1024].rearrange("p (k t) -> p k t", t=128)
        for ti in range(4):
            c0, n = TILES[ti]
            pb, pk = bank()
            for k in range(8):
                sq, sk = RB.next()
                act(sq[:], xT[:, k, c0:c0 + n], AF.Square, [("x", k, ti)], [sk])
                S.op("pe", MM(pb[:], onesb[:], sq[:], k == 0, k == 7), reads=[sk, "onesb"], writes=[pk])
            rs, rk = rstd_from(pb[:], n, 1.0 / D, pk)
            for s4 in range(4):
                i = ti * 4 + s4
                cs = slice(c0 + s4 * 128, c0 + (s4 + 1) * 128)
                for k in range(8):
                    stt(YT[:, k, :], xT[:, k, cs], FG[:, k:k + 1], rs[:, s4 * 128:(s4 + 1) * 128], ALU.mult, ALU.mult, [("x", k, ti), "FG", rk, ("YT", k)], [("YT", k)])
                xo, xok = xin[i % 2], ("xin", i % 2)
                for half in range(2):
                    pt, ptk = bank()
                    for j in range(4):
                        k = half * 4 + j
                        pe_T(pt[:, j * 128:(j + 1) * 128], YT[:, k, :], ident[:], [("YT", k), "ident"], [ptk])
                    cp(xo[:, half * 512:(half + 1) * 512], pt[:], [ptk], [xok], eng=("act" if half else "dve"))
                S.dma("sp", y_p[i * 128:(i + 1) * 128, :], xo[:], reads=[xok], slot=("yo", i % 2))
        norm_sample(tmps[:], "tmps2", 0, 0, final=True)
        pt, ptk = bank()
        pt2, ptk2 = bank()
        for k in range(8):
            tgt = pt if k < 4 else pt2
            tk_ = ptk if k < 4 else ptk2
            pe_T(tgt[0:NS, (k % 4) * 128:(k % 4 + 1) * 128], tmps[:, k, :], ident[:], ["tmps2", "ident"], [tk_])
        cp(xin[0][0:NS, 0:512], pt[0:NS, :], [ptk], [("xin", 0)])
        cp(xin[0][0:NS, 512:1024], pt2[0:NS, :], [ptk2], [("xin", 0)])
        S.dma("sp", y_s, xin[0][0:NS, :], reads=[("xin", 0)], slot=("yo", 0))

        S.wait_all("sp")
        S.emit()
    return nc


_NC_CACHE = {}


def _shard_inputs(inputs):
    f = lambda a: np.ascontiguousarray(np.asarray(a, dtype=np.float32))
    consts = make_consts()
    shared = {k: f(inputs[k]) for k in ("w_ada", "b_ada", "w_in", "lb_logits", "a_norm_g", "pool_map", "pool_scale", "c_ln_g", "c_ln_b",
                                        "c_ws", "c_bs", "w_br_a", "w_br_b", "w_br_c", "w_out", "w_ffn_in", "w_ffn_out")}
    shared["final_g"] = f(inputs["final_g"]).reshape(8, 128)
    shared["consts"] = consts
    x_prompt = f(inputs["x_prompt"]); x_sample = f(inputs["x_sample"])
    state_hgrn = f(inputs["state_hgrn"]); state_pool = f(inputs["state_pool"])
    c_prompt = f(inputs["c_prompt"]); c_sample = f(inputs["c_sample"])
    maps = []
    for c in range(8):
        sl = slice(c * NS, (c + 1) * NS)
        m = dict(shared)
        m["xp"] = x_prompt[c]
        m["xs"] = np.ascontiguousarray(x_sample[sl, 0, :])
        m["sh"] = np.ascontiguousarray(state_hgrn[:, sl])
        m["spl"] = np.ascontiguousarray(state_pool[:, sl])
        m["cc"] = np.ascontiguousarray(np.concatenate([c_prompt[c:c + 1], c_sample[sl]], axis=0))
        maps.append(m)
    return maps


def kernel(**inputs):
    if "nc" not in _NC_CACHE:
        _NC_CACHE["nc"] = build()
    nc = _NC_CACHE["nc"]
    maps = _shard_inputs(inputs)
    res = run_bass_kernel_spmd(nc, maps, core_ids=list(range(8)))
    r = res.results
    y_prompt = np.stack([r[c]["y_p"] for c in range(8)], 0)
    y_sample = np.concatenate([r[c]["y_s"] for c in range(8)], 0)[:, None, :]
    hgrn_prompt = np.stack([r[c]["hg_p"] for c in range(8)], 1)
    pool_prompt = np.stack([r[c]["pl_p"] for c in range(8)], 1)
    hgrn_sample = np.concatenate([r[c]["hg_s"] for c in range(8)], 1)
    pool_sample = np.concatenate([r[c]["pl_s"] for c in range(8)], 1)
    chunk_v = np.concatenate([r[c]["cv_s"] for c in range(8)], 1)[:, :, None, :]
    return (y_prompt.astype(np.float32), y_sample.astype(np.float32), hgrn_prompt.astype(np.float32), pool_prompt.astype(np.float32),
            hgrn_sample.astype(np.float32), pool_sample.astype(np.float32), chunk_v.astype(np.float32))
```

```python
import contextlib
import numpy as np
import concourse.bass as bass
import concourse.mybir as mybir
from concourse.bass_utils import run_bass_kernel_spmd

F32 = mybir.dt.float32
BF16 = mybir.dt.bfloat16
U8 = mybir.dt.uint8
AF = mybir.ActivationFunctionType
ALU = mybir.AluOpType
AX = mybir.AxisListType

SEM_EPOCH = 30000
SAME_ENGINE_SYNC = True

D = 1024
NPT = 2048
NS = 16
NT = NPT + NS
DEPTH = 4
IN_COLS = 5888
DFF = 2816
EPS = 1e-6
NW = 6
WSLOT = 2048
FFN_ROUNDS = [(0, 4), (4, 4), (8, 4), (12, 4), (16, 3), (19, 3)]
NCONST = 546


class Sched:
    ENG = ("pe", "act", "dve", "pool", "sp")

    def __init__(self, nc, stack):
        self.nc = nc
        self.stack = stack
        self.prog = {e: [] for e in self.ENG}
        self.cur = {e: None for e in self.ENG}
        self.seen = {e: {} for e in self.ENG}
        self.lastw = {}
        self.readers = {}
        self.nsem = 0
        self.slot_sems = {}
        self.ninst = {e: 0 for e in self.ENG}
        self.dead = set()

    def new_sem(self, name):
        self.nsem += 1
        return self.stack.enter_context(self.nc.semaphore(f"{name}_{self.nsem}"))

    def _wait(self, e, tok):
        if tok is None:
            return
        sem, val, owner = tok
        if owner == e and not SAME_ENGINE_SYNC:
            return
        if self.seen[e].get(sem, 0) >= val:
            return
        self.seen[e][sem] = val
        self.prog[e].append(lambda eng, sem=sem, val=val: eng.wait_ge(sem, val))

    def retire(self, olds, new):
        if not isinstance(olds, list):
            olds = [olds]
        toks = []
        for old in olds:
            lw = self.lastw.pop(old, None)
            if lw is not None:
                toks.append(lw)
            toks += self.readers.pop(old, [])
            self.dead.add(old)
        self.readers[new] = toks

    def _deps(self, e, reads, writes, own_sem=None):
        for k in list(reads) + list(writes):
            assert k not in self.dead, f"stale buffer key used after re-allocation: {k}"
        need = {}

        def want(tok):
            if tok is None:
                return
            sem, val, owner = tok
            if own_sem is not None and sem is own_sem:
                return
            if sem not in need or need[sem][1] < val:
                need[sem] = tok
        for k in reads:
            want(self.lastw.get(k))
            if k in ("psl", "psb") or (isinstance(k, tuple) and k and k[0] == "ps"):
                for t in self.readers.get(k, ()):
                    if t[2] != e:
                        want(t)
        for k in writes:
            want(self.lastw.get(k))
            for t in self.readers.get(k, ()):
                want(t)
        for tok in need.values():
            self._wait(e, tok)

    def _commit(self, tok, reads, writes):
        for k in reads:
            self.readers.setdefault(k, []).append(tok)
        for k in writes:
            self.lastw[k] = tok
            self.readers[k] = []

    def _eng_token(self, e):
        c = self.cur[e]
        if c is None or c[1] >= SEM_EPOCH:
            c = [self.new_sem("s" + e), 0]
            self.cur[e] = c
        c[1] += 1
        return (c[0], c[1], e)

    def op(self, e, fns, reads=(), writes=()):
        if callable(fns):
            fns = [fns]
        self._deps(e, reads, writes)
        tok = self._eng_token(e)
        n = len(fns)
        for i, fn in enumerate(fns):
            if i == n - 1:
                self.prog[e].append(lambda eng, fn=fn, tok=tok: fn(eng).then_inc(tok[0], 1))
            else:
                self.prog[e].append(lambda eng, fn=fn: fn(eng))
        self.ninst[e] += n
        self._commit(tok, reads, writes)
        return tok

    def dma(self, q, out, in_, reads=(), writes=(), slot=None, **kw):
        if slot is None:
            slot = ("auto", tuple(writes) if writes else tuple(reads))
        s = self.slot_sems.get(slot)
        if s is None or s[1] + 16 > SEM_EPOCH:
            s = [self.new_sem("d"), 0]
            self.slot_sems[slot] = s
        self._deps(q, reads, writes, own_sem=(s[0] if writes else None))
        s[1] += 16
        tok = (s[0], s[1], "dma")
        self.prog[q].append(lambda eng, tok=tok, out=out, in_=in_, kw=kw:
                            eng.dma_start(out=out, in_=in_, **kw).then_inc(tok[0], 16))
        self.ninst[q] += 1
        self._commit(tok, reads, writes)
        return tok

    def barrier(self, engines=("pe", "act", "dve", "sp")):
        toks = []
        for e in self.ENG:
            c = self.cur[e]
            if c is not None and c[1] > 0:
                toks.append((c[0], c[1], e))
        for s in self.slot_sems.values():
            toks.append((s[0], s[1], "dma"))
        for e in engines:
            for t in toks:
                if t[2] == e:
                    continue
                self._wait(e, t)

    def wait_all(self, e="sp"):
        for k, t in list(self.lastw.items()):
            self._wait(e, t)
        for k, ts in list(self.readers.items()):
            for t in ts:
                self._wait(e, t)

    def emit(self):
        nc = self.nc
        with nc.Block() as block:
            @block.tensor
            def _(eng):
                for t in self.prog["pe"]:
                    t(eng)

            @block.scalar
            def _(eng):
                for t in self.prog["act"]:
                    t(eng)

            @block.vector
            def _(eng):
                for t in self.prog["dve"]:
                    t(eng)

            @block.gpsimd
            def _(eng):
                for t in self.prog["pool"]:
                    t(eng)

            @block.sync
            def _(eng):
                for t in self.prog["sp"]:
                    t(eng)


class Ring:
    sched = None

    def __init__(self, name, aps):
        self.name = name
        self.aps = aps
        self.i = 0
        self.gen = [0] * len(aps)

    def next(self):
        i = self.i % len(self.aps)
        self.i += 1
        old = (self.name, i, self.gen[i])
        self.gen[i] += 1
        new = (self.name, i, self.gen[i])
        Ring.sched.retire([old] + [(old, j) for j in range(2)], new)
        return self.aps[i], new


def make_consts():
    c = np.zeros((128, NCONST), np.float32)
    c[:, 0:128] = np.eye(128, dtype=np.float32)
    s = np.arange(128)[:, None]
    t = np.arange(128)[None, :]
    c[:, 128:256] = ((s // 64 == t // 64) & (s <= t)).astype(np.float32)
    c[:, 256:384] = (s <= t).astype(np.float32)
    w_of = lambda p, cc: [2, 4, 8, 16][2 * cc + p // 64]
    for p in range(128):
        for cc in range(2):
            w = w_of(p, cc)
            for tt in range(16):
                c[p, 384 + cc * 16 + tt] = 1.0 / min(tt + 1, w)
            c[p, 416 + cc] = 1.0 / w
    for g in range(4):
        w = [2, 4, 8, 16][g]
        for t in range(2):
            for bb in range(8):
                for r in range(15):
                    if r >= 16 - w:
                        c[bb * 15 + r, 418 + (g * 2 + t) * 16 + bb + 8 * t] = 1.0
    return c


def build(nl=DEPTH, debug=False):
    nc = bass.Bass("TRN2", target_bir_lowering=False)
    din = lambda name, shape: nc.dram_tensor(name, list(shape), F32, kind="ExternalInput").ap()
    dout = lambda name, shape: nc.dram_tensor(name, list(shape), F32, kind="ExternalOutput").ap()
    xp = din("xp", [NPT, D])
    xs = din("xs", [NS, D])
    sh = din("sh", [DEPTH, NS, 4, 128, 128])
    spl = din("spl", [DEPTH, NS, 15, 256])
    cc = din("cc", [17, D])
    w_ada = din("w_ada", [DEPTH, D, 6 * D])
    b_ada = din("b_ada", [DEPTH, 6 * D])
    w_in = din("w_in", [DEPTH, D, IN_COLS])
    lb_logits = din("lb_logits", [DEPTH, 512])
    a_norm_g = din("a_norm_g", [DEPTH, 4, 128])
    pool_map = din("pool_map", [DEPTH, 4, 64, 64])
    pool_scale = din("pool_scale", [DEPTH, 256])
    c_ln_g = din("c_ln_g", [DEPTH, 256])
    c_ln_b = din("c_ln_b", [DEPTH, 256])
    c_ws = din("c_ws", [DEPTH, 4, 128, 128])
    c_bs = din("c_bs", [DEPTH, 4, 128])
    w_br_a = din("w_br_a", [DEPTH, 512, D])
    w_br_b = din("w_br_b", [DEPTH, 256, D])
    w_br_c = din("w_br_c", [DEPTH, 256, D])
    w_out = din("w_out", [DEPTH, D, D])
    w_ffn_in = din("w_ffn_in", [DEPTH, D, 2 * DFF])
    w_ffn_out = din("w_ffn_out", [DEPTH, DFF, D])
    final_g = din("final_g", [8, 128])
    consts = din("consts", [128, NCONST])

    y_p = dout("y_p", [NPT, D])
    y_s = dout("y_s", [NS, D])
    hg_p = dout("hg_p", [DEPTH, 4, 128, 128])
    pl_p = dout("pl_p", [DEPTH, 15, 256])
    hg_s = dout("hg_s", [DEPTH, NS, 4, 128, 128])
    pl_s = dout("pl_s", [DEPTH, NS, 15, 256])
    cv_s = dout("cv_s", [DEPTH, NS, 256])
    dbg = {}
    if debug:
        dbg["x"] = dout("dbg_x", [DEPTH + 1, 128, 8, NT])
        dbg["m"] = dout("dbg_m", [DEPTH, 128, 48, 17])

    with contextlib.ExitStack() as st:
        S = Sched(nc, st)
        Ring.sched = S
        sb = lambda name, shape, dt=F32: st.enter_context(nc.sbuf_tensor(name, list(shape), dt))
        psum = lambda name, shape, dt=F32: st.enter_context(nc.psum_tensor(name, list(shape), dt))

        xT = sb("xT", [128, 8, NT])
        AB = sb("arena_b", [128, 25600], BF16)
        AFt = sb("arena_f", [128, 3072])
        WR = [sb(f"wr{i}", [128, WSLOT], BF16) for i in range(NW)]
        modTs = [sb(f"modT{i}", [128, 48, 17]) for i in range(2)]
        modT, MK = modTs[0], ("modT", 0)
        condT = sb("condT", [128, 8, 17], BF16)
        ident = sb("ident", [128, 128])
        identb = sb("identb", [128, 128], BF16)
        onesb = sb("onesb", [128, 128], BF16)
        scmask = sb("scmask", [128, 4, 128], U8)
        trimask = sb("trimask", [128, 4, 128], U8)
        selw = sb("selw", [128, 128])
        rmask = sb("rmask", [128, 512])
        zeros = sb("zeros", [128, 128])
        rc16 = sb("rc16", [128, 2, 16])
        invw = sb("invw", [128, 2])
        LBv = sb("LBv", [128, 4, 4])
        OML = sb("OML", [128, 4, 4])
        NOML = sb("NOML", [128, 4, 4])
        LBB = sb("LBB", [128, 4, 4])
        ANG = sb("ANG", [128, 16])
        FG = sb("FG", [128, 8])
        PSC = sb("PSC", [128, 8])
        BADAs = [sb(f"BADA{i}", [128, 48]) for i in range(2)]
        bsT = sb("bsT", [128, 2, 128])
        w00 = sb("w00", [128, 2])
        b00 = sb("b00", [128, 2])
        lng = sb("lng", [128, 256])
        lnb = sb("lnb", [128, 256])
        BD = sb("BD", [128, 2, 128], BF16)
        WmT = sb("WmT", [128, 4, 128], BF16)
        Sst = sb("Sst", [128, 4, 128])
        Sb = [sb(f"Sb{i}", [128, 4, 128], BF16) for i in range(2)]
        EB = sb("EB", [128, 4, 8])
        RF = Ring("rf", [sb(f"rf{i}", [128, 512]) for i in range(4)])
        RS = Ring("rs", [sb(f"rs{i}", [128, 512]) for i in range(2)])
        RB = Ring("rb", [sb(f"rbf{i}", [128, 512], BF16) for i in range(2)])
        QS = [sb(f"qs{i}", [128, 512]) for i in range(1)]
        KK = [sb(f"kk{i}", [128, 512]) for i in range(1)]
        BB = [sb(f"bb{i}", [128, 512]) for i in range(1)]
        MACC = sb("macc", [128, 512])
        hTs = sb("hTs", [128, 8, NS], BF16)
        Vs = sb("Vs", [NS, 512])
        vns = sb("vns", [NS, 256])
        xnf = sb("xnf", [128, 2, NS])
        sS0 = [sb(f"sS0{i}", [128, 4, 128]) for i in range(2)]
        qs_s = sb("qs_s", [128, 4, NS])
        f_s = sb("f_s", [128, 4, NS])
        kk_s = sb("kk_s", [128, 4, NS])
        GSs = sb("GSs", [128, 4, NS])
        OTs = sb("OTs", [128, 4, NS], BF16)
        ZBs = sb("ZBs", [128, 2, NS], BF16)
        PBMs = sb("PBMs", [128, 2, NS], BF16)
        us_s = sb("us_s", [128, 2, NS])
        GMs = sb("GMs", [128, 2, NS], BF16)
        MTs = sb("MTs", [128, 8, NS], BF16)
        maccs = sb("maccs", [128, NS])
        tmps = sb("tmps", [128, 8, NS])
        tail = sb("tail", [NS, 256])
        stat = sb("stat", [128, 24])
        NEGH = sb("NEGH", [128, 1])

        o = 0
        def carve(n):
            nonlocal o
            v = AB[:, o:o + n]
            o += n
            return v
        hTg = carve(8 * 512).rearrange("p (k t) -> p k t", t=512)
        QX = carve(8 * 512).rearrange("p (h t) -> p h t", t=512)
        QH = QX[:, 0:4, :]
        QT = QX[:, 4:8, :]
        MT = QX
        KT = carve(4 * 512).rearrange("p (h t) -> p h t", t=512)
        KHtok = carve(4 * 512).rearrange("p (h s k) -> p h s k", s=4, k=128)
        Vtok = carve(4 * 512).rearrange("p (s c) -> p s c", c=512)
        GS = carve(4 * 512).rearrange("p (h t) -> p h t", t=512)
        OT = carve(4 * 512).rearrange("p (h t) -> p h t", t=512)
        ZB = carve(2 * 512).rearrange("p (c t) -> p c t", t=512)
        PBM = carve(2 * 512).rearrange("p (c t) -> p c t", t=512)
        VN = carve(4 * 256).rearrange("p (s c) -> p s c", c=256)
        GM = carve(2 * 512).rearrange("p (c t) -> p c t", t=512)
        KHTb = carve(512)
        scT = [carve(512).rearrange("p (h t) -> p h t", t=128) for _ in range(3)]
        assert o <= 25600, o
        h2T = AB[:, 0:8 * NT].rearrange("p (k t) -> p k t", t=NT)
        ACTB = AB[:, 8 * NT:12 * NT].rearrange("p (j t) -> p j t", t=NT)
        assert 12 * NT <= 25600
        PBt = AFt[:, 0:1054].rearrange("p (c t) -> p c t", t=527)
        PA = AFt[:, 1054:1581]
        PBb = AFt[:, 1581:2108]
        xin = [AFt[:, 1024:2048], AFt[:, 2048:3072]]
        smallin = AFt[:, 0:512]
        Xa = AFt[:, 2108:2364]
        Xb = AFt[:, 2364:2620]
        pnew = AFt[0:NS, 2620:2876]
        RSN = sb("RSN", [128, 512])

        PSF = [psum(f"psf{i}", [128, 512]) for i in range(6)]
        PSL = psum("psl", [128, 512])
        PSB = psum("psb", [128, 1024], BF16)
        psi = [0]

        def MM(out, lhsT, rhs, start=True, stop=True, skip=False):
            return lambda e: e.matmul(out, lhsT=lhsT, rhs=rhs, start=start, stop=stop, skip_group_check=skip)

        def SEL(out, mask, on_true, on_false):
            return lambda e: e.select(out=out, mask=mask, on_true=on_true, on_false=on_false, add_drain=True)

        pgen = [0] * 6

        def bank():
            i = psi[0] % 6
            psi[0] += 1
            old = ("ps", i, pgen[i])
            pgen[i] += 1
            new = ("ps", i, pgen[i])
            S.retire(old, new)
            return PSF[i], new

        wslot_i = [0]

        def wload(pieces):
            i = wslot_i[0] % NW
            wslot_i[0] += 1
            keys = [("w", i, j) for j in range(6)]
            views = []
            off = 0
            for j, src in enumerate(pieces):
                rows, cols = src.shape
                kch = rows // 128
                v = WR[i][:, off:off + kch * cols].rearrange("p (k n) -> p k n", n=cols)
                S.dma("pool", v, src.rearrange("(k p) n -> p k n", p=128), reads=(), writes=[keys[j]], slot=("w", i))
                views.append(v)
                off += kch * cols
            assert off <= WSLOT
            return views, keys

        def mm_group(out, pairs, reads, writes, extra=()):
            n = len(pairs)
            fns = [MM(out, l, r, i == 0, i == n - 1) for i, (l, r) in enumerate(pairs)]
            return S.op("pe", fns, reads=reads, writes=writes)

        def pe_T(out_ps, in_sb, idn, reads, writes):
            return S.op("pe", lambda e: e.transpose(out=out_ps, in_=in_sb, identity=idn), reads=reads, writes=writes)

        def act(out, in_, func, reads, writes, **kw):
            return S.op("act", lambda e: e.activation(out=out, in_=in_, func=func, **kw), reads=reads, writes=writes)

        def tt(out, in0, in1, op, reads, writes, eng="dve"):
            return S.op(eng, lambda e: e.tensor_tensor(out=out, in0=in0, in1=in1, op=op), reads=reads, writes=writes)

        def ts(out, in0, s1, s2, op0, op1, reads, writes, eng="dve"):
            if s2 is None:
                return S.op(eng, lambda e: e.tensor_scalar(out=out, in0=in0, scalar1=s1, scalar2=None, op0=op0), reads=reads, writes=writes)
            return S.op(eng, lambda e: e.tensor_scalar(out=out, in0=in0, scalar1=s1, scalar2=s2, op0=op0, op1=op1), reads=reads, writes=writes)

        def stt(out, in0, scalar, in1, op0, op1, reads, writes, eng="dve"):
            return S.op(eng, lambda e: e.scalar_tensor_tensor(out=out, in0=in0, scalar=scalar, in1=in1, op0=op0, op1=op1), reads=reads, writes=writes)

        def cp(out, in_, reads, writes, eng="dve"):
            if eng == "act":
                return act(out, in_, AF.Copy, reads, writes)
            return S.op(eng, lambda e: e.tensor_copy(out=out, in_=in_), reads=reads, writes=writes)

        def memset(ap, val, writes, eng="dve"):
            return S.op(eng, lambda e: e.memset(ap, val), writes=writes)

        def recip(out, in_, reads, writes):
            return S.op("dve", lambda e: e.reciprocal(out=out, in_=in_), reads=reads, writes=writes)

        def small_T(src_rows_ap, nrows, dst, dst_key, name):
            S.dma("sp", smallin[0:nrows, 0:128], src_rows_ap, writes=["PBt"], slot="PBt")
            pb, pk = bank()
            pe_T(pb[:, 0:nrows], smallin[0:nrows, 0:128], ident[0:nrows, 0:nrows], ["PBt", "ident"], [pk])
            cp(dst, pb[:, 0:nrows], [pk], [dst_key])

        xkeys = lambda ti: [("x", k, ti) for k in range(8)]

        S.dma("sp", ident[:], consts[:, 0:128], writes=["ident"])
        S.dma("sp", smallin[:, 0:256], consts[:, 128:384], writes=["PBt"], slot="PBt")
        S.dma("sp", selw[:], consts[:, 418:546], writes=["selw"])
        S.dma("sp", rc16[:], consts[:, 384:416].rearrange("p (c t) -> p c t", t=16), writes=["rc16"])
        S.dma("sp", invw[:], consts[:, 416:418], writes=["invw"])
        cp(identb[:], ident[:], ["ident"], ["identb"])
        memset(onesb[:], 1.0, ["onesb"])
        memset(zeros[:], 0.0, ["zeros"])
        memset(NEGH[:], -0.5, ["NEGH"])
        memset(rmask[:], 1.0, ["rmask"])
        memset(rmask[:].rearrange("p (c t) -> p c t", t=64)[:, :, 0:1], 0.0, ["rmask"])
        cp(scmask[:], smallin[:, 0:128].unsqueeze(1).to_broadcast([128, 4, 128]), ["PBt"], ["scmask"])
        cp(trimask[:], smallin[:, 128:256].unsqueeze(1).to_broadcast([128, 4, 128]), ["PBt"], ["trimask"])
        memset(BD[:], 0.0, ["BD"])

        S.dma("sp", xin[1][0:17, :], cc, writes=[("xin", 1)], slot=("xin", 1))
        act(xin[1][0:17, :], xin[1][0:17, :], AF.Silu, [("xin", 1)], [("xin", 1)])
        pb, pk = bank()
        for k in range(8):
            pe_T(pb[:, k * 17:(k + 1) * 17], xin[1][0:17, k * 128:(k + 1) * 128], ident[0:17, 0:17], [("xin", 1), "ident"], [pk])
        cp(condT[:], pb[:, 0:8 * 17].rearrange("p (k t) -> p k t", t=17), [pk], ["condT"])
        small_T(a_norm_g.rearrange("l h v -> (l h) v"), 16, ANG[:], "ANG", "ang")
        small_T(final_g, 8, FG[:], "FG", "fg")
        small_T(pool_scale.rearrange("l (c p) -> (l c) p", p=128), 8, PSC[:], "PSC", "psc")
        S.dma("sp", smallin[0:4, 0:512], lb_logits, writes=["PBt"], slot="PBt")
        pb, pk = bank()
        for h in range(4):
            pe_T(pb[:, h * 4:(h + 1) * 4], smallin[0:4, h * 128:(h + 1) * 128], ident[0:4, 0:4], ["PBt", "ident"], [pk])
        lg = OML
        cp(lg[:], pb[:, 0:16].rearrange("p (h l) -> p h l", l=4), [pk], ["OML"])
        S.op("dve", lambda e: e.tensor_reduce(out=stat[:, 0:4], in_=lg[:], axis=AX.X, op=ALU.max), reads=["OML"], writes=["stat"])
        tt(lg[:], lg[:], stat[:, 0:4].unsqueeze(2).to_broadcast([128, 4, 4]), ALU.subtract, ["OML", "stat"], ["OML"])
        act(lg[:], lg[:], AF.Exp, ["OML"], ["OML"])
        S.op("dve", lambda e: e.tensor_reduce(out=stat[:, 4:8], in_=lg[:], axis=AX.X, op=ALU.add), reads=["OML"], writes=["stat"])
        recip(stat[:, 4:8], stat[:, 4:8], ["stat"], ["stat"])
        tt(lg[:], lg[:], stat[:, 4:8].unsqueeze(2).to_broadcast([128, 4, 4]), ALU.mult, ["OML", "stat"], ["OML"])
        memset(LBv[:], 0.0, ["LBv"])
        cp(LBv[:, :, 1:2], lg[:, :, 1:2], ["OML"], ["LBv"])
        tt(LBv[:, :, 2:3], LBv[:, :, 1:2], lg[:, :, 2:3], ALU.add, ["OML", "LBv"], ["LBv"])
        tt(LBv[:, :, 3:4], LBv[:, :, 2:3], lg[:, :, 3:4], ALU.add, ["OML", "LBv"], ["LBv"])
        ts(LBv[:], LBv[:], 0.0, None, ALU.max, None, ["LBv"], ["LBv"])
        ts(LBB[:], LBv[:], 1e-30, None, ALU.add, None, ["LBv"], ["LBB"])
        ts(NOML[:], LBv[:], -1.0, None, ALU.add, None, ["LBv"], ["NOML"])
        ts(OML[:], LBv[:], -1.0, 1.0, ALU.mult, ALU.add, ["LBv"], ["OML"])
        TILES = [(i * 512, 512) for i in range(4)]

        def rstd_from(ps_ap, n, scale, pk):
            rs, rk = RS.next()
            act(rs[:, 0:n], ps_ap, AF.Ln, [pk], [rk], scale=scale, bias=EPS)
            act(rs[:, 0:n], rs[:, 0:n], AF.Exp, [rk], [rk], scale=-0.5)
            return rs, rk

        def norm_stats(ti):
            c0, n = TILES[ti]
            pb, pk = bank()
            for k in range(8):
                sq, sk = RB.next()
                act(sq[:], xT[:, k, c0:c0 + n], AF.Square, [("x", k, ti)], [sk])
                S.op("pe", MM(pb[:], onesb[:], sq[:], k == 0, k == 7), reads=[sk, "onesb"], writes=[pk])
            act(RSN[:], pb[:], AF.Ln, [pk], ["RSN"], scale=1.0 / D, bias=EPS)
            act(RSN[:], RSN[:], AF.Exp, ["RSN"], ["RSN"], scale=-0.5)

        def norm_apply(ti, dst, dst_keys, a0, s0):
            c0, n = TILES[ti]
            for k in range(8):
                t1, tk = RF.next()
                stt(t1[:], xT[:, k, c0:c0 + n], modT[:, a0 + k, 0:1], RSN[:], ALU.mult, ALU.mult, [("x", k, ti), MK, "RSN"], [tk])
                act(dst(k), t1[:], AF.Identity, [tk, MK], [dst_keys[k]], bias=modT[:, s0 + k, 0:1], scale=1.0)

        def norm_prompt(ti, dst, dst_keys, a0, s0):
            c0, n = TILES[ti]
            pb, pk = bank()
            for k in range(8):
                act(dst(k), xT[:, k, c0:c0 + n], AF.Square, [("x", k, ti)], [dst_keys[k]])
                S.op("pe", MM(pb[:], onesb[:], dst(k), k == 0, k == 7), reads=[dst_keys[k], "onesb"], writes=[pk])
            rs, rk = rstd_from(pb[:], n, 1.0 / D, pk)
            for k in range(8):
                t1, tk = RF.next()
                stt(t1[:], xT[:, k, c0:c0 + n], modT[:, a0 + k, 0:1], rs[:], ALU.mult, ALU.mult, [("x", k, ti), MK, rk], [tk])
                act(dst(k), t1[:], AF.Identity, [tk, MK], [dst_keys[k]], bias=modT[:, s0 + k, 0:1], scale=1.0)

        def norm_sample(dst, dst_key, a0, s0, final=False):
            sq, sk = RB.next()
            sqv = sq[:, 0:8 * NS].rearrange("p (k t) -> p k t", t=NS)
            act(sqv, xT[:, :, NPT:NT], AF.Square, xkeys(4), [sk])
            pb, pk = bank()
            mm_group(pb[:, 0:NS], [(onesb[:], sqv[:, k, :]) for k in range(8)], [sk, "onesb"], [pk])
            rs, rk = rstd_from(pb[:, 0:NS], NS, 1.0 / D, pk)
            tt(tmps[:], xT[:, :, NPT:NT], rs[:, 0:NS].unsqueeze(1).to_broadcast([128, 8, NS]), ALU.mult, xkeys(4) + [rk], ["tmps"])
            if final:
                tt(dst, tmps[:], FG[:].unsqueeze(2).to_broadcast([128, 8, NS]), ALU.mult, ["tmps", "FG"], [dst_key])
            else:
                tt(tmps[:], tmps[:], modT[:, a0:a0 + 8, 1:17], ALU.mult, ["tmps", MK], ["tmps"])
                tt(dst, tmps[:], modT[:, s0:s0 + 8, 1:17], ALU.add, ["tmps", MK], [dst_key])

        def ada_blocks(lx):
            dstT, dk = modTs[lx % 2], ("modT", lx % 2)
            bada, bk_ = BADAs[lx % 2], ("BADA", lx % 2)
            small_T(b_ada[lx].rearrange("(c p) -> c p", p=128), 48, bada[:], bk_, "bada")
            for j in range(24):
                (wv,), wk = wload([w_ada[lx][:, j * 256:(j + 1) * 256]])
                pb, pk = bank()
                for c4 in range(2):
                    mm_group(pb[:, c4 * 17:(c4 + 1) * 17], [(wv[:, k, c4 * 128:(c4 + 1) * 128], condT[:, k, :]) for k in range(8)], wk + ["condT"], [pk])
                tt(dstT[:, 2 * j:2 * j + 2, :], pb[:, 0:34].rearrange("p (c t) -> p c t", t=17),
                   bada[:, 2 * j:2 * j + 2].unsqueeze(2).to_broadcast([128, 2, 17]), ALU.add, [pk, bk_], [(dk, j)])
                if j == 23:
                    allk = [(dk, jj) for jj in range(24)]
                    ts(dstT[:, 8:16, :], dstT[:, 8:16, :], 1.0, None, ALU.add, None, allk, [dk])
                    ts(dstT[:, 32:40, :], dstT[:, 32:40, :], 1.0, None, ALU.add, None, [dk], [dk])
                    if debug:
                        S.dma("sp", dbg["m"][lx], dstT[:], reads=[dk])
                yield j

        def layer_params(l):
            for g in range(4):
                S.dma("sp", bsT[(g % 2) * 64:(g % 2 + 1) * 64, g // 2, :], c_bs[l, g:g + 1, :].partition_broadcast(64), writes=[("bsT", g)])
                S.dma("sp", w00[(g % 2) * 64:(g % 2 + 1) * 64, g // 2:g // 2 + 1], c_ws[l, g, 0:1, 0:1].partition_broadcast(64), writes=[("w00", g)])
                S.dma("sp", b00[(g % 2) * 64:(g % 2 + 1) * 64, g // 2:g // 2 + 1], c_bs[l, g:g + 1, 0:1].partition_broadcast(64), writes=[("b00", g)])
                S.dma("pool", BD[(g % 2) * 64:(g % 2 + 1) * 64, g // 2, (g % 2) * 64:(g % 2 + 1) * 64], pool_map[l, g], reads=["BD"], writes=[("BDg", g)], slot=("bd", g))
            S.dma("sp", lng[:], c_ln_g[l:l + 1, :].partition_broadcast(128), writes=["lng"])
            S.dma("sp", lnb[:], c_ln_b[l:l + 1, :].partition_broadcast(128), writes=["lnb"])
            S.dma("sp", smallin[:], c_ws[l].rearrange("g t s -> t g s"), writes=["PBt"], slot="PBt")
            pb, pk = bank()
            for g in range(4):
                pe_T(pb[:, g * 128:(g + 1) * 128], smallin[:, g * 128:(g + 1) * 128], ident[:], ["PBt", "ident"], [pk])
            S.op("dve", SEL(WmT[:], trimask[:], pb[:].rearrange("p (g t) -> p g t", t=128), zeros[:].unsqueeze(1).to_broadcast([128, 4, 128])),
                 reads=[pk, "trimask", "zeros"], writes=["WmT"])
            memset(Sst[:], 0.0, ["Sst"])
            memset(Sb[1][:], 0.0, [("Sb", 1)])

        BSK = [("bsT", g) for g in range(4)]
        W0K = [("w00", g) for g in range(4)] + [("b00", g) for g in range(4)]
        BDK = ["BD"] + [("BDg", g) for g in range(4)]

        def tm_group(l, gi):
            c0, n = TILES[gi]
            smp = (gi == 3)
            HK = [("h", k) for k in range(8)]
            Wl = w_in[l]
            if gi == 0:
                norm_stats(0)
                norm_apply(0, lambda k: hTg[:, k, :], HK, 8, 0)
            if smp:
                norm_sample(hTs[:], "hTs", 8, 0)

            for jb in range(2):
                (wv,), wk = wload([Wl[:, 1024 + jb * 256:1024 + (jb + 1) * 256]])
                for s4 in range(4):
                    pb, pk = bank()
                    mm_group(pb[:, 0:256], [(hTg[:, k, s4 * 128:(s4 + 1) * 128], wv[:, k, :]) for k in range(8)], wk + HK, [pk])
                    cp(Vtok[:, s4, jb * 256:(jb + 1) * 256], pb[:, 0:256], [pk], [("Vtok", s4, jb)], eng=("act" if s4 % 2 == 0 else "dve"))
                if smp:
                    pb, pk = bank()
                    mm_group(pb[0:NS, 0:256], [(hTs[:, k, :], wv[:, k, :]) for k in range(8)], wk + ["hTs"], [pk])
                    cp(Vs[:, jb * 256:(jb + 1) * 256], pb[0:NS, 0:256], [pk], [("Vs", jb)])
            VK = lambda s4: [("Vtok", s4, 0), ("Vtok", s4, 1)]

            (wvp,), wkp = wload([Wl[:, 2048:2304]])
            if gi == 0:
                memset(PBt[:, :, 0:15], 0.0, ["PBt"])
            else:
                for c2 in range(2):
                    cp(PA[:, 0:15], PBt[:, c2, 512:527], ["PBt"], ["PA"])
                    cp(PBt[:, c2, 0:15], PA[:, 0:15], ["PA"], ["PBt"])
            for c2 in range(2):
                pb, pk = bank()
                mm_group(pb[:], [(wvp[:, k, c2 * 128:(c2 + 1) * 128], hTg[:, k, :]) for k in range(8)], wkp + HK, [pk])
                cp(PBt[:, c2, 15:527], pb[:], [pk], ["PBt"], eng="act")
            if smp:
                pb, pk = bank()
                mm_group(pb[0:NS, 0:256], [(hTg[:, k, 496:512], wvp[:, k, :]) for k in range(8)], wkp + HK, [pk])
                cp(tail[:], pb[0:NS, 0:256], [pk], ["tail"])
                S.dma("sp", pl_p[l], tail[1:16, :], reads=["tail"], slot="plp")
                S.dma("sp", Xa[0:120, :], spl[l, 0:8].rearrange("b r c -> (b r) c"), writes=["Xa"], slot="Xa")
                S.dma("sp", Xb[0:120, :], spl[l, 8:16].rearrange("b r c -> (b r) c"), writes=["Xb"], slot="Xb")
                S.dma("sp", pl_s[l, :, 0:14, :], spl[l, :, 1:15, :], reads=[("dramcopy", l)], slot="plsd")
                pb, pk = bank()
                mm_group(pb[0:NS, 0:256], [(hTs[:, k, :], wvp[:, k, :]) for k in range(8)], wkp + ["hTs"], [pk])
                cp(pnew[:], pb[0:NS, 0:256], [pk], ["pnew"])
                S.dma("sp", pl_s[l, :, 14, :], pnew[:], reads=["pnew"], slot="pls")
                pb, pk = bank()
                for c2 in range(2):
                    mm_group(pb[:, c2 * NS:(c2 + 1) * NS], [(wvp[:, k, c2 * 128:(c2 + 1) * 128], hTs[:, k, :]) for k in range(8)], wkp + ["hTs"], [pk])
                cp(xnf[:], pb[:, 0:2 * NS].rearrange("p (c t) -> p c t", t=NS), [pk], ["xnf"])
                pz, pzk = bank()
                fns = []
                for c2 in range(2):
                    for gh in range(2):
                        g = 2 * c2 + gh
                        o_ = pz[gh * 64:(gh + 1) * 64, c2 * NS:(c2 + 1) * NS]
                        fns.append(MM(o_, Xa[0:120, g * 64:(g + 1) * 64], selw[0:120, (g * 2) * 16:(g * 2 + 1) * 16], True, False))
                        fns.append(MM(o_, Xb[0:120, g * 64:(g + 1) * 64], selw[0:120, (g * 2 + 1) * 16:(g * 2 + 2) * 16], False, True))
                S.op("pe", fns, reads=["Xa", "Xb", "selw"], writes=[pzk])
                for c2 in range(2):
                    t1, tk = RF.next()
                    tt(t1[:, 0:NS], pz[:, c2 * NS:(c2 + 1) * NS], xnf[:, c2, :], ALU.add, [pzk, "xnf"], [tk])
                    stt(ZBs[:, c2, :], t1[:, 0:NS], invw[:, c2:c2 + 1], xnf[:, c2, :], ALU.mult, ALU.subtract, [tk, "invw", "xnf"], ["ZBs"])

            def zb_from(src, c2, lo, hi, key):
                stt(ZB[lo:hi, c2, :], src[lo:hi, 15:527], invw[lo:hi, c2:c2 + 1], PBt[lo:hi, c2, 15:527], ALU.mult, ALU.subtract,
                    [key, "invw", "PBt"], [("ZB", c2, lo)])
                if gi == 0:
                    t1, tk = RF.next()
                    tt(t1[lo:hi, 0:16], src[lo:hi, 15:31], rc16[lo:hi, c2, :], ALU.mult, [key, "rc16"], [tk])
                    tt(ZB[lo:hi, c2, 0:16], t1[lo:hi, 0:16], PBt[lo:hi, c2, 15:31], ALU.subtract, [tk, "PBt", ("ZB", c2, lo)], [("ZB", c2, lo)])
            tt(PA[:, 1:527], PBt[:, 0, 1:527], PBt[:, 0, 0:526], ALU.add, ["PBt"], ["PA"])
            zb_from(PA, 0, 0, 64, "PA")
            tt(PBb[:, 3:527], PA[:, 3:527], PA[:, 1:525], ALU.add, ["PA"], ["PBb"])
            zb_from(PBb, 0, 64, 128, "PBb")
            tt(PA[:, 1:527], PBt[:, 1, 1:527], PBt[:, 1, 0:526], ALU.add, ["PBt"], ["PA"])
            tt(PBb[:, 3:527], PA[:, 3:527], PA[:, 1:525], ALU.add, ["PA"], ["PBb"])
            tt(PA[:, 7:527], PBb[:, 7:527], PBb[:, 3:523], ALU.add, ["PBb"], ["PA"])
            zb_from(PA, 1, 0, 64, "PA")
            tt(PBb[:, 15:527], PA[:, 15:527], PA[:, 7:519], ALU.add, ["PA"], ["PBb"])
            zb_from(PBb, 1, 64, 128, "PBb")
            ZBK = [("ZB", c2, lo) for c2 in range(2) for lo in (0, 64)]

            (wv,), wk = wload([Wl[:, 2560:2816]])
            gl = []
            for s4 in range(4):
                pb, pk = bank()
                mm_group(pb[:, 0:256], [(hTg[:, k, s4 * 128:(s4 + 1) * 128], wv[:, k, :]) for k in range(8)], wk + HK, [pk])
                g, gk = RF.next()
                act(g[:, 0:256], pb[:, 0:256], AF.Gelu_apprx_tanh, [pk], [gk])
                gl.append((g[:, 0:256], gk, 128))
            if smp:
                pb, pk = bank()
                mm_group(pb[0:NS, 0:256], [(hTs[:, k, :], wv[:, k, :]) for k in range(8)], wk + ["hTs"], [pk])
                act(vns[:], pb[0:NS, 0:256], AF.Gelu_apprx_tanh, [pk], ["vns"])
                gl.append((vns[:], "vns", NS))
            memset(stat[:, 6:16], 1.0, ["stat"])
            for i, (g, gk, np_) in enumerate(gl):
                S.op("dve", lambda e, g=g, np_=np_: e.bn_stats(out=stat[0:np_, 0:6], in_=g[0:np_, :]), reads=[gk, "stat"], writes=["stat"])
                S.op("dve", lambda e, i=i, np_=np_: e.bn_aggr(out=stat[0:np_, 6 + 2 * i:8 + 2 * i], in_=stat[0:np_, 0:6]), reads=["stat"], writes=["stat"])
            var5 = stat[:, 6:16].rearrange("p (i t) -> p i t", t=2)[:, :, 1]
            rstd5 = stat[:, 16:21]
            act(rstd5, var5, AF.Ln, ["stat"], ["stat"], scale=1.0, bias=EPS)
            act(rstd5, rstd5, AF.Exp, ["stat"], ["stat"], scale=-0.5)
            for i, (g, gk, np_) in enumerate(gl):
                ts(g[0:np_, :], g[0:np_, :], stat[0:np_, 6 + 2 * i:7 + 2 * i], stat[0:np_, 16 + i:17 + i], ALU.subtract, ALU.mult, [gk, "stat"], [gk])
                tt(g[0:np_, :], g[0:np_, :], lng[0:np_, :], ALU.mult, [gk, "lng"], [gk])
                if i < 4:
                    tt(VN[:, i, :], g[0:np_, :], lnb[0:np_, :], ALU.add, [gk, "lnb"], [("VN", i)])
                else:
                    tt(vns[:], vns[:], lnb[0:NS, :], ALU.add, ["vns", "lnb"], ["vns"])
                    S.dma("sp", cv_s[l], vns[:], reads=["vns"], slot="cv")

            for jb in range(2):
                (wv,), wk = wload([Wl[:, 1536 + jb * 256:1536 + (jb + 1) * 256]])
                for hh in range(2):
                    h = jb * 2 + hh
                    pb, pk = bank()
                    mm_group(pb[:], [(wv[:, k, hh * 128:(hh + 1) * 128], hTg[:, k, :]) for k in range(8)], wk + HK, [pk])
                    t1, tk = RF.next()
                    act(t1[:], pb[:], AF.Silu, [pk], [tk])
                    ts(GS[:, h, :], t1[:], ANG[:, l * 4 + h:l * 4 + h + 1], None, ALU.mult, None, [tk, "ANG"], [("GS", h)])
                if smp:
                    pb, pk = bank()
                    for hh in range(2):
                        mm_group(pb[:, hh * NS:(hh + 1) * NS], [(wv[:, k, hh * 128:(hh + 1) * 128], hTs[:, k, :]) for k in range(8)], wk + ["hTs"], [pk])
                    act(GSs[:, jb * 2:jb * 2 + 2, :], pb[:, 0:2 * NS].rearrange("p (h t) -> p h t", t=NS), AF.Silu, [pk], [("GSs", jb)])
                    for hh in range(2):
                        h = jb * 2 + hh
                        ts(GSs[:, h, :], GSs[:, h, :], ANG[:, l * 4 + h:l * 4 + h + 1], None, ALU.mult, None, [("GSs", jb), "ANG"], [("GSs", jb)])
            GSsK = [("GSs", 0), ("GSs", 1)]

            oml = lambda h: OML[:, h, l:l + 1]
            noml = lambda h: NOML[:, h, l:l + 1]
            lbb = lambda h: LBB[:, h, l:l + 1]
            LK = ["OML", "NOML", "LBB"]
            qk, kkk, bk = ("QS", 0), ("KK", 0), ("BB", 0)
            qs, kk, b_ = QS[0], KK[0], BB[0]
            wqf = {}

            def qf_mm(h):
                jb, hh = h // 2, h % 2
                if hh == 0:
                    (wq,), wkq = wload([Wl[:, jb * 256:(jb + 1) * 256]])
                    (wf,), wkf = wload([Wl[:, 512 + jb * 256:512 + (jb + 1) * 256]])
                    wqf[jb] = (wq, wkq, wf, wkf)
                wq, wkq, wf, wkf = wqf[jb]
                hc = slice(hh * 128, (hh + 1) * 128)
                pq, pqk = bank()
                mm_group(pq[:], [(wq[:, k, hc], hTg[:, k, :]) for k in range(8)], wkq + HK, [pqk])
                pf, pfk = bank()
                mm_group(pf[:], [(wf[:, k, hc], hTg[:, k, :]) for k in range(8)], wkf + HK, [pfk])
                ps_ = None
                if smp:
                    ps_, psk = bank()
                    mm_group(ps_[:, 0:NS], [(wq[:, k, hc], hTs[:, k, :]) for k in range(8)], wkq + ["hTs"], [psk])
                    mm_group(ps_[:, NS:2 * NS], [(wf[:, k, hc], hTs[:, k, :]) for k in range(8)], wkf + ["hTs"], [psk])
                    ps_ = (ps_, psk)
                return pq, pqk, pf, pfk, ps_

            def sigmoid_lnexp(dst, src, n, rk, wk_):
                act(dst, src, AF.Exp, rk, [wk_], scale=-1.0)
                act(dst, dst, AF.Ln, [wk_], [wk_], bias=1.0, scale=1.0)
                act(dst, dst, AF.Exp, [wk_], [wk_], scale=-1.0)

            (wv,), wk = wload([Wl[:, 2304:2560]])
            for c2 in range(2):
                pb, pk = bank()
                mm_group(pb[:], [(wv[:, k, c2 * 128:(c2 + 1) * 128], hTg[:, k, :]) for k in range(8)], wk + HK, [pk])
                pb2, pk2 = bank()
                fns = []
                for s4 in range(4):
                    for gh in range(2):
                        g = 2 * c2 + gh
                        fns.append(MM(pb2[gh * 64:(gh + 1) * 64, s4 * 128:(s4 + 1) * 128], VN[:, s4, g * 64:(g + 1) * 64], WmT[:, g, :]))
                S.op("pe", fns, reads=[("VN", s4) for s4 in range(4)] + ["WmT"], writes=[pk2])
                ug, uk = RF.next()
                act(ug[:], pb[:], AF.Gelu_apprx_tanh, [pk], [uk])
                t1, tk = RF.next()
                tt(t1[:].rearrange("p (s t) -> p s t", t=128), pb2[:].rearrange("p (s t) -> p s t", t=128),
                   bsT[:, c2, :].unsqueeze(1).to_broadcast([128, 4, 128]), ALU.add, [pk2] + BSK, [tk])
                tt(GM[:, c2, :], t1[:], ug[:], ALU.mult, [tk, uk], [("GM", c2)])
            if smp:
                pb, pk = bank()
                for c2 in range(2):
                    mm_group(pb[:, c2 * NS:(c2 + 1) * NS], [(wv[:, k, c2 * 128:(c2 + 1) * 128], hTs[:, k, :]) for k in range(8)], wk + ["hTs"], [pk])
                act(us_s[:], pb[:, 0:2 * NS].rearrange("p (c t) -> p c t", t=NS), AF.Gelu_apprx_tanh, [pk], ["us_s"])
                pb, pk = bank()
                for c2 in range(2):
                    pe_T(pb[:, c2 * NS:(c2 + 1) * NS], vns[:, c2 * 128:(c2 + 1) * 128], ident[0:NS, 0:NS], ["vns", "ident"], [pk])
                for c2 in range(2):
                    t1, tk = RF.next()
                    ts(t1[:, 0:NS], pb[:, c2 * NS:(c2 + 1) * NS], w00[:, c2:c2 + 1], b00[:, c2:c2 + 1], ALU.mult, ALU.add, [pk] + W0K, [tk])
                    tt(GMs[:, c2, :], t1[:, 0:NS], us_s[:, c2, :], ALU.mult, [tk, "us_s"], ["GMs"])

            for c2 in range(2):
                pb, pk = bank()
                mm_group(pb[:], [(BD[:, c2, :], ZB[:, c2, :])], BDK + ZBK, [pk])
                act(PBM[:, c2, :], pb[:], AF.Identity, [pk, "PSC"], [("PBM", c2)], scale=PSC[:, l * 2 + c2:l * 2 + c2 + 1], bias=0.0)
            if smp:
                pb, pk = bank()
                for c2 in range(2):
                    mm_group(pb[:, c2 * NS:(c2 + 1) * NS], [(BD[:, c2, :], ZBs[:, c2, :])], BDK + ["ZBs"], [pk])
                for c2 in range(2):
                    act(PBMs[:, c2, :], pb[:, c2 * NS:(c2 + 1) * NS], AF.Identity, [pk, "PSC"], ["PBMs"], scale=PSC[:, l * 2 + c2:l * 2 + c2 + 1], bias=0.0)

            nxt = qf_mm(0)
            pending = []
            NHF = 1
            HW_, HC_ = 512 // NHF, 8 // NHF
            HF = [(i, slice(i * HW_, (i + 1) * HW_)) for i in range(NHF)]
            for h in range(4):
                pq, pqk, pf, pfk, ps_ = nxt
                if h + 1 < 4:
                    nxt = qf_mm(h + 1)
                a1, a1k = RF.next()
                sg, sgk = RF.next()
                for hf, cs in HF:
                    act(a1[:, cs], pq[:, cs], AF.Exp, [pqk], [(a1k, hf), a1k], scale=-1.0)
                for hf, cs in HF:
                    act(sg[:, cs], pf[:, cs], AF.Exp, [pfk], [(sgk, hf), sgk], scale=-1.0)
                for t_, tk_ in ((a1, a1k), (sg, sgk)):
                    for hf, cs in HF:
                        act(t_[:, cs], t_[:, cs], AF.Ln, [(tk_, hf)], [(tk_, hf)], bias=1.0, scale=1.0)
                    for hf, cs in HF:
                        act(t_[:, cs], t_[:, cs], AF.Exp, [(tk_, hf)], [(tk_, hf)], scale=-1.0)
                while pending:
                    pending.pop()()
                lf, lfk = RF.next()
                for hf, cs in HF:
                    tt(qs[:, cs], pq[:, cs], a1[:, cs], ALU.mult, [pqk, (a1k, hf)], [(qk, hf)])
                    act(lf[:, cs], sg[:, cs], AF.Ln, [(sgk, hf)] + LK, [(lfk, hf), lfk], scale=oml(h), bias=lbb(h))
                for hf, cs in HF:
                    ts(kk[:, cs], sg[:, cs], noml(h), oml(h), ALU.mult, ALU.add, [(sgk, hf)] + LK, [(kkk, hf)])
                    S.op("dve", lambda e, lf=lf, cs=cs: e.tensor_tensor_scan(out=b_[:, cs], data0=rmask[:, cs], data1=lf[:, cs], initial=0.0,
                                                                             op0=ALU.mult, op1=ALU.add),
                         reads=["rmask", (lfk, hf)], writes=[(bk, hf)])
                d1, d1k = RF.next()
                d3, d3k = RF.next()
                b3 = lambda t, cs: t[:, cs].rearrange("p (c t) -> p c t", t=64)
                for hf, cs in HF:
                    act(EB[:, h, HC_ * hf:HC_ * hf + HC_], b3(b_, cs)[:, :, 63], AF.Exp, [(bk, hf)], [("EB", h, hf)])
                    tt(b3(d1, cs), b3(b_, cs), b3(b_, cs)[:, :, 31:32].to_broadcast([128, HC_, 64]), ALU.subtract, [(bk, hf)], [(d1k, hf), d1k])
                    tt(b3(d3, cs), b3(b_, cs)[:, :, 63:64].to_broadcast([128, HC_, 64]), b3(b_, cs), ALU.subtract, [(bk, hf)], [(d3k, hf), d3k])
                e0, e0k = RF.next()
                e1, e1k = RF.next()
                for hf, cs in HF:
                    act(e0[:, cs], b_[:, cs], AF.Exp, [(bk, hf)], [(e0k, hf), e0k])
                    act(e1[:, cs], d1[:, cs], AF.Exp, [(d1k, hf)], [(e1k, hf), e1k])
                for hf, cs in HF:
                    act(d1[:, cs], d1[:, cs], AF.Exp, [(d1k, hf)], [(d1k, hf)], scale=-1.0)
                    act(d3[:, cs], d3[:, cs], AF.Exp, [(d3k, hf)], [(d3k, hf)])
                for hf, cs in HF:
                    tt(QH[:, h, cs], qs[:, cs], e0[:, cs], ALU.mult, [(qk, hf), (e0k, hf)], [("QX", h, hf), ("QX", h)])
                    tt(QT[:, h, cs], qs[:, cs], e1[:, cs], ALU.mult, [(qk, hf), (e1k, hf)], [("QX", 4 + h, hf), ("QX", 4 + h)])
                for hf, cs in HF:
                    tt(KT[:, h, cs], kk[:, cs], d1[:, cs], ALU.mult, [(kkk, hf), (d1k, hf)], [("KT", h, hf)])
                    tt(KHTb[:, cs], kk[:, cs], d3[:, cs], ALU.mult, [(kkk, hf), (d3k, hf)], [("KHTb", hf)])
                for s4 in range(4):
                    pe_T(PSB[:, s4 * 128:(s4 + 1) * 128], KHTb[:, s4 * 128:(s4 + 1) * 128], identb[:], [("KHTb", 0), ("KHTb", 1), "identb"], ["psb"])
                pending.append(lambda h=h: cp(KHtok[:, h, :, :], PSB[:, 0:512].rearrange("p (s k) -> p s k", k=128), ["psb"],
                                              [("KHtok", h, 0), ("KHtok", h, 1)], eng="act"))
                if smp:
                    pb, pk = ps_
                    sg2, sg2k = RF.next()
                    sigmoid_lnexp(sg2[:, 0:2 * NS], pb[:, 0:2 * NS], 2 * NS, [pk], sg2k)
                    tt(qs_s[:, h, :], pb[:, 0:NS], sg2[:, 0:NS], ALU.mult, [pk, sg2k], [("qs_s", h)])
                    ts(kk_s[:, h, :], sg2[:, NS:2 * NS], noml(h), oml(h), ALU.mult, ALU.add, [sg2k] + LK, [("kk_s", h)])
                    ts(f_s[:, h, :], sg2[:, NS:2 * NS], oml(h), lbb(h), ALU.mult, ALU.add, [sg2k] + LK, [("f_s", h)])
            while pending:
                pending.pop()()

            QHK = [("QX", h, hf) for h in range(4) for hf in range(2)]
            QTK = [("QX", 4 + h, hf) for h in range(4) for hf in range(2)]
            KTK = [("KT", h, hf) for h in range(4) for hf in range(2)]
            KHK = [("KHtok", h, hf) for h in range(4) for hf in range(2)]
            GSK = [("GS", h) for h in range(4)]
            EBK = [("EB", h, hf) for h in range(4) for hf in range(2)]
            S4v = lambda t: t[:].rearrange("p (h t) -> p h t", t=128)

            def scores(s4):
                cs = slice(s4 * 128, (s4 + 1) * 128)
                pc, pck = bank()
                fns = [MM(pc[:, h * 128:(h + 1) * 128], KT[:, h, cs], QT[:, h, cs]) for h in range(4)]
                S.op("pe", fns, reads=KTK + QTK, writes=[pck])
                sc, sck = scT[s4 % 3], ("scT", s4 % 3)
                S.op("dve", SEL(sc, scmask[:], S4v(pc), zeros[:].unsqueeze(1).to_broadcast([128, 4, 128])), reads=[pck, "scmask", "zeros"], writes=[sck])

            def state_update(c):
                s4, half = c // 2, c % 2
                lo, hi = half * 64, half * 64 + 64
                pd, pdk = bank()
                fns = [MM(pd[:, h * 128:(h + 1) * 128], KHtok[lo:hi, h, s4, :], Vtok[lo:hi, s4, h * 128:(h + 1) * 128]) for h in range(4)]
                S.op("pe", fns, reads=KHK + VK(s4), writes=[pdk])
                tt(Sst[:], Sst[:], EB[:, :, c:c + 1].to_broadcast([128, 4, 128]), ALU.mult, ["Sst"] + EBK, ["Sst"])
                tt(Sst[:], Sst[:], S4v(pd), ALU.add, ["Sst", pdk], ["Sst"])
                cp(Sb[c % 2][:], Sst[:], ["Sst"], [("Sb", c % 2)], eng="act")

            def sample_hgrn():
                SK = [("qs_s", h) for h in range(4)]
                po, pok = PSL, "psl"

                def bcast(b):
                    s0, s0k = sS0[b % 2], ("sS0", b % 2)
                    S.dma("sp", s0[:], sh[l, b].rearrange("h k v -> k h v"), writes=[s0k], slot=s0k)
                    pv, pvk = bank()
                    S.op("pe", MM(pv[:], ident[0:NS, b:b + 1].to_broadcast([NS, 128]), Vs[:]), reads=["ident", ("Vs", 0), ("Vs", 1)], writes=[pvk])
                    return pv, pvk
                nxtv = bcast(0)
                for b in range(NS):
                    pv, pvk = nxtv
                    s0, s0k = sS0[b % 2], ("sS0", b % 2)
                    sn, snk = s0, s0k
                    for h in range(4):
                        act(s0[:, h, :], s0[:, h, :], AF.Identity, [s0k, ("f_s", h)], [s0k], scale=f_s[:, h, b:b + 1], bias=0.0)
                    for h in range(4):
                        stt(s0[:, h, :], pv[:, h * 128:(h + 1) * 128], kk_s[:, h, b:b + 1], s0[:, h, :], ALU.mult, ALU.add,
                            [pvk, ("kk_s", h), s0k], [s0k])
                    if b + 1 < NS:
                        nxtv = bcast(b + 1)
                    S.dma("sp", hg_s[l, b].rearrange("h k v -> k h v"), sn[:], reads=[snk], slot=("hgs", b % 2))
                    fns = [MM(po[:, h * NS + b:h * NS + b + 1], sn[:, h, :], qs_s[:, h, b:b + 1]) for h in range(4)]
                    S.op("pe", fns, reads=[snk] + SK, writes=[pok])
                    yield b
                sq, sqk = RB.next()
                act(sq[:, 0:4 * NS], po[:, 0:4 * NS], AF.Square, [pok], [sqk])
                pn, pnk = bank()
                mm_group(pn[:, 0:4 * NS], [(onesb[:], sq[:, 0:4 * NS])], [sqk, "onesb"], [pnk])
                rs, rk = RS.next()
                act(rs[:, 0:4 * NS], pn[:, 0:4 * NS], AF.Ln, [pnk], [rk], scale=1.0 / 128, bias=EPS)
                act(rs[:, 0:4 * NS], rs[:, 0:4 * NS], AF.Exp, [rk], [rk], scale=-0.5)
                t1, tk = RF.next()
                tt(t1[:, 0:4 * NS], po[:, 0:4 * NS], rs[:, 0:4 * NS], ALU.mult, [pok, rk], [tk])
                tt(OTs[:], t1[:, 0:4 * NS].rearrange("p (h t) -> p h t", t=NS), GSs[:], ALU.mult, [tk] + GSsK, ["OTs"])

            smp_it = sample_hgrn() if smp else iter(())
            scores(0)
            scores(1)
            state_update(0)
            for s4 in range(4):
                cs = slice(s4 * 128, (s4 + 1) * 128)
                sc, sck = scT[s4 % 3], ("scT", s4 % 3)
                po, pok = bank()
                fns = []
                for h in range(4):
                    fns.append(MM(po[:, h * 128:(h + 1) * 128], Vtok[:, s4, h * 128:(h + 1) * 128], sc[:, h, :], True, False, True))
                    fns.append(MM(po[:, h * 128:h * 128 + 64], Sb[1][:, h, :], QH[:, h, s4 * 128:s4 * 128 + 64], False, False, True))
                    fns.append(MM(po[:, h * 128 + 64:(h + 1) * 128], Sb[0][:, h, :], QH[:, h, s4 * 128 + 64:(s4 + 1) * 128], False, True, True))
                S.op("pe", fns, reads=VK(s4) + [sck, ("Sb", 0), ("Sb", 1)] + QHK, writes=[pok])
                state_update(2 * s4 + 1)
                if s4 + 1 < 4:
                    state_update(2 * s4 + 2)
                if s4 + 2 < 4:
                    scores(s4 + 2)
                sq, sqk = RB.next()
                act(sq[:], po[:], AF.Square, [pok], [sqk])
                pn, pnk = bank()
                mm_group(pn[:], [(onesb[:], sq[:])], [sqk, "onesb"], [pnk])
                rs, rk = RS.next()
                act(rs[:], pn[:], AF.Ln, [pnk], [rk], scale=1.0 / 128, bias=EPS)
                act(rs[:], rs[:], AF.Exp, [rk], [rk], scale=-0.5)
                t1, tk = RF.next()
                tt(t1[:], po[:], rs[:], ALU.mult, [pok, rk], [tk])
                tt(OT[:, :, cs], S4v(t1), GS[:, :, cs], ALU.mult, [tk] + GSK, [("OT", s4)])
                for _ in range(4):
                    next(smp_it, None)
            OTK = [("OT", s4) for s4 in range(4)]
            if gi == 3:
                S.dma("sp", hg_p[l].rearrange("h k v -> k h v"), Sst[:], reads=["Sst"], slot="hgp")
            for _ in smp_it:
                pass

            if gi + 1 < 4:
                norm_stats(gi + 1)

            PBMK = [("PBM", c2) for c2 in range(2)]
            GMK = [("GM", c2) for c2 in range(2)]
            for d in range(8):
                dc = slice(d * 128, (d + 1) * 128)
                specs = [(2816, w_br_a, OT, 4, OTK, OTs, "OTs"), (3840, w_br_b, PBM, 2, PBMK, PBMs, "PBMs"), (4864, w_br_c, GM, 2, GMK, GMs, "GMs")]
                for pi, (gc0, wbr, src, nk, srck, srcs, srcsk) in enumerate(specs):
                    (wg, wy), wk = wload([Wl[:, gc0 + d * 128:gc0 + (d + 1) * 128], wbr[l][:, dc]])
                    pg, pgk = bank()
                    mm_group(pg[:], [(wg[:, k, :], hTg[:, k, :]) for k in range(8)], wk + HK, [pgk])
                    py, pyk = bank()
                    mm_group(py[:], [(wy[:, j, :], src[:, j, :]) for j in range(nk)], wk + srck, [pyk])
                    sg, sgk = RF.next()
                    act(sg[:], pg[:], AF.Sigmoid, [pgk], [sgk])
                    if pi == 0:
                        tt(MACC[:], sg[:], py[:], ALU.mult, [sgk, pyk], ["MACC"])
                    else:
                        t1, tk = RF.next()
                        tt(t1[:], sg[:], py[:], ALU.mult, [sgk, pyk], [tk])
                        if pi == 1:
                            tt(MACC[:], MACC[:], t1[:], ALU.add, ["MACC", tk], ["MACC"])
                        else:
                            tt(MT[:, d, :], MACC[:], t1[:], ALU.add, ["MACC", tk], [("QX", d), ("QX", d, 0), ("QX", d, 1)])
                    if smp:
                        pb, pk = bank()
                        mm_group(pb[:, 0:NS], [(wg[:, k, :], hTs[:, k, :]) for k in range(8)], wk + ["hTs"], [pk])
                        mm_group(pb[:, NS:2 * NS], [(wy[:, j, :], srcs[:, j, :]) for j in range(nk)], wk + [srcsk], [pk])
                        sg, sgk = RF.next()
                        act(sg[:, 0:NS], pb[:, 0:NS], AF.Sigmoid, [pk], [sgk])
                        if pi == 0:
                            tt(maccs[:], sg[:, 0:NS], pb[:, NS:2 * NS], ALU.mult, [sgk, pk], ["maccs"])
                        else:
                            tt(sg[:, 0:NS], sg[:, 0:NS], pb[:, NS:2 * NS], ALU.mult, [sgk, pk], [sgk])
                            if pi == 1:
                                tt(maccs[:], maccs[:], sg[:, 0:NS], ALU.add, ["maccs", sgk], ["maccs"])
                            else:
                                tt(MTs[:, d, :], maccs[:], sg[:, 0:NS], ALU.add, ["maccs", sgk], ["MTs"])

            if gi + 1 < 4:
                norm_apply(gi + 1, lambda k: hTg[:, k, :], HK, 8, 0)

            MTK = [("QX", d) for d in range(8)]
            for jb in range(4):
                (wv,), wk = wload([w_out[l][:, jb * 256:(jb + 1) * 256]])
                for d2 in range(2):
                    dd = jb * 2 + d2
                    pb, pk = bank()
                    mm_group(pb[:], [(wv[:, k, d2 * 128:(d2 + 1) * 128], MT[:, k, :]) for k in range(8)], wk + MTK, [pk])
                    stt(xT[:, dd, c0:c0 + n], pb[:], modT[:, 16 + dd, 0:1], xT[:, dd, c0:c0 + n], ALU.mult, ALU.add, [pk, MK, ("x", dd, gi)], [("x", dd, gi)])
                    if smp:
                        pb, pk = bank()
                        mm_group(pb[:, 0:NS], [(wv[:, k, d2 * 128:(d2 + 1) * 128], MTs[:, k, :]) for k in range(8)], wk + ["MTs"], [pk])
                        t1, tk = RF.next()
                        tt(t1[:, 0:NS], pb[:, 0:NS], modT[:, 16 + dd, 1:17], ALU.mult, [pk, MK], [tk])
                        tt(xT[:, dd, NPT:NT], xT[:, dd, NPT:NT], t1[:, 0:NS], ALU.add, [tk, ("x", dd, 4)], [("x", dd, 4)])

        ALLT = TILES + [(NPT, NS)]

        def ffn(l):
            for ti in range(4):
                c0, n = TILES[ti]
                norm_prompt(ti, lambda k, c0=c0, n=n: h2T[:, k, c0:c0 + n], [("h2", k, ti) for k in range(8)], 32, 24)
            norm_sample(h2T[:, :, NPT:NT], ("h2s",), 32, 24)
            h2k = lambda ti: ([("h2", k, ti) for k in range(8)] if ti < 4 else [("h2s",)])
            Wi = w_ffn_in[l]
            Wo = w_ffn_out[l]
            ada_it = ada_blocks(l + 1) if l + 1 < nl else iter(())
            for (j0, nj) in FFN_ROUNDS:
                for jj in range(nj):
                    j = j0 + jj
                    next(ada_it, None)
                    (wg, wu), wk = wload([Wi[:, j * 128:(j + 1) * 128], Wi[:, DFF + j * 128:DFF + (j + 1) * 128]])
                    for ti, (c0, n) in enumerate(ALLT):
                        pg, pgk = bank()
                        mm_group(pg[:, 0:n], [(wg[:, k, :], h2T[:, k, c0:c0 + n]) for k in range(8)], wk + h2k(ti), [pgk])
                        pu, puk = bank()
                        mm_group(pu[:, 0:n], [(wu[:, k, :], h2T[:, k, c0:c0 + n]) for k in range(8)], wk + h2k(ti), [puk])
                        sl, slk = RF.next()
                        act(sl[:, 0:n], pg[:, 0:n], AF.Silu, [pgk], [slk])
                        tt(ACTB[:, jj, c0:c0 + n], sl[:, 0:n], pu[:, 0:n], ALU.mult, [slk, puk], [("actb", jj, ti)])
                for dp in range(4):
                    if dp == 0:
                        next(ada_it, None)
                    views, wk = wload([Wo[j0 * 128:(j0 + nj) * 128, (2 * dp + q) * 128:(2 * dp + q + 1) * 128] for q in range(2)])
                    for q in range(2):
                        dd = 2 * dp + q
                        wv = views[q]
                        for ti, (c0, n) in enumerate(ALLT):
                            pb, pk = bank()
                            mm_group(pb[:, 0:n], [(wv[:, j, :], ACTB[:, j, c0:c0 + n]) for j in range(nj)], wk + [("actb", j, ti) for j in range(nj)], [pk])
                            if ti < 4:
                                stt(xT[:, dd, c0:c0 + n], pb[:, 0:n], modT[:, 40 + dd, 0:1], xT[:, dd, c0:c0 + n], ALU.mult, ALU.add,
                                    [pk, MK, ("x", dd, ti)], [("x", dd, ti)])
                            else:
                                t1, tk = RF.next()
                                tt(t1[:, 0:NS], pb[:, 0:NS], modT[:, 40 + dd, 1:17], ALU.mult, [pk, MK], [tk])
                                tt(xT[:, dd, NPT:NT], xT[:, dd, NPT:NT], t1[:, 0:NS], ALU.add, [tk, ("x", dd, 4)], [("x", dd, 4)])

        ada_it0 = ada_blocks(0)
        for i in range(NPT // 128):
            next(ada_it0, None)
            xi = xin[i % 2]
            xk = ("xin", i % 2)
            S.dma("sp", xi[:], xp[i * 128:(i + 1) * 128, :], writes=[xk], slot=xk)
            ti = i // 4
            for half in range(2):
                pb, pk = bank()
                for j in range(4):
                    k = half * 4 + j
                    pe_T(pb[:, j * 128:(j + 1) * 128], xi[:, k * 128:(k + 1) * 128], ident[:], [xk, "ident"], [pk])
                cp(xT[:, half * 4:half * 4 + 4, i * 128:(i + 1) * 128], pb[:].rearrange("p (j t) -> p j t", t=128),
                   [pk], [("x", half * 4 + j, ti) for j in range(4)], eng=("act" if half else "dve"))
        S.dma("sp", xin[0][0:NS, :], xs, writes=[("xin", 0)], slot=("xin", 0))
        pb, pk = bank()
        for k in range(8):
            pe_T(pb[:, k * NS:(k + 1) * NS], xin[0][0:NS, k * 128:(k + 1) * 128], ident[0:NS, 0:NS], [("xin", 0), "ident"], [pk])
        cp(xT[:, :, NPT:NT], pb[:, 0:8 * NS].rearrange("p (k t) -> p k t", t=NS), [pk], xkeys(4))
        for _ in ada_it0:
            pass
        if debug:
            S.dma("sp", dbg["x"][0], xT[:], reads=[k for ti in range(5) for k in xkeys(ti)])

        S.barrier()
        for l in range(nl):
            modT, MK = modTs[l % 2], ("modT", l % 2)
            layer_params(l)
            for gi in range(4):
                tm_group(l, gi)
            S.barrier()
            ffn(l)
            S.barrier()
            if debug:
                S.dma("sp", dbg["x"][l + 1], xT[:], reads=[k for ti in range(5) for k in xkeys(ti)])

        YT = AFt[:, 0:1024].rearrange("p (k t) -> p k t", t=128)
        for ti in range(4):
            c0, n = TILES[ti]
            pb, pk = bank()
            for k in range(8):
                sq, sk = RB.next()
                act(sq[:], xT[:, k, c0:c0 + n], AF.Square, [("x", k, ti)], [sk])
                S.op("pe", MM(pb[:], onesb[:], sq[:], k == 0, k == 7), reads=[sk, "onesb"], writes=[pk])
            rs, rk = rstd_from(pb[:], n, 1.0 / D, pk)
            for s4 in range(4):
                i = ti * 4 + s4
                cs = slice(c0 + s4 * 128, c0 + (s4 + 1) * 128)
                for k in range(8):
                    stt(YT[:, k, :], xT[:, k, cs], FG[:, k:k + 1], rs[:, s4 * 128:(s4 + 1) * 128], ALU.mult, ALU.mult, [("x", k, ti), "FG", rk, ("YT", k)], [("YT", k)])
                xo, xok = xin[i % 2], ("xin", i % 2)
                for half in range(2):
                    pt, ptk = bank()
                    for j in range(4):
                        k = half * 4 + j
                        pe_T(pt[:, j * 128:(j + 1) * 128], YT[:, k, :], ident[:], [("YT", k), "ident"], [ptk])
                    cp(xo[:, half * 512:(half + 1) * 512], pt[:], [ptk], [xok], eng=("act" if half else "dve"))
                S.dma("sp", y_p[i * 128:(i + 1) * 128, :], xo[:], reads=[xok], slot=("yo", i % 2))
        norm_sample(tmps[:], "tmps2", 0, 0, final=True)
        pt, ptk = bank()
        pt2, ptk2 = bank()
        for k in range(8):
            tgt = pt if k < 4 else pt2
            tk_ = ptk if k < 4 else ptk2
            pe_T(tgt[0:NS, (k % 4) * 128:(k % 4 + 1) * 128], tmps[:, k, :], ident[:], ["tmps2", "ident"], [tk_])
        cp(xin[0][0:NS, 0:512], pt[0:NS, :], [ptk], [("xin", 0)])
        cp(xin[0][0:NS, 512:1024], pt2[0:NS, :], [ptk2], [("xin", 0)])
        S.dma("sp", y_s, xin[0][0:NS, :], reads=[("xin", 0)], slot=("yo", 0))

        S.wait_all("sp")
        S.emit()
    return nc


_NC_CACHE = {}


def _shard_inputs(inputs):
    f = lambda a: np.ascontiguousarray(np.asarray(a, dtype=np.float32))
    consts = make_consts()
    shared = {k: f(inputs[k]) for k in ("w_ada", "b_ada", "w_in", "lb_logits", "a_norm_g", "pool_map", "pool_scale", "c_ln_g", "c_ln_b",
                                        "c_ws", "c_bs", "w_br_a", "w_br_b", "w_br_c", "w_out", "w_ffn_in", "w_ffn_out")}
    shared["final_g"] = f(inputs["final_g"]).reshape(8, 128)
    shared["consts"] = consts
    x_prompt = f(inputs["x_prompt"]); x_sample = f(inputs["x_sample"])
    state_hgrn = f(inputs["state_hgrn"]); state_pool = f(inputs["state_pool"])
    c_prompt = f(inputs["c_prompt"]); c_sample = f(inputs["c_sample"])
    maps = []
    for c in range(8):
        sl = slice(c * NS, (c + 1) * NS)
        m = dict(shared)
        m["xp"] = x_prompt[c]
        m["xs"] = np.ascontiguousarray(x_sample[sl, 0, :])
        m["sh"] = np.ascontiguousarray(state_hgrn[:, sl])
        m["spl"] = np.ascontiguousarray(state_pool[:, sl])
        m["cc"] = np.ascontiguousarray(np.concatenate([c_prompt[c:c + 1], c_sample[sl]], axis=0))
        maps.append(m)
    return maps


def kernel(**inputs):
    if "nc" not in _NC_CACHE:
        _NC_CACHE["nc"] = build()
    nc = _NC_CACHE["nc"]
    maps = _shard_inputs(inputs)
    res = run_bass_kernel_spmd(nc, maps, core_ids=list(range(8)))
    r = res.results
    y_prompt = np.stack([r[c]["y_p"] for c in range(8)], 0)
    y_sample = np.concatenate([r[c]["y_s"] for c in range(8)], 0)[:, None, :]
    hgrn_prompt = np.stack([r[c]["hg_p"] for c in range(8)], 1)
    pool_prompt = np.stack([r[c]["pl_p"] for c in range(8)], 1)
    hgrn_sample = np.concatenate([r[c]["hg_s"] for c in range(8)], 1)
    pool_sample = np.concatenate([r[c]["pl_s"] for c in range(8)], 1)
    chunk_v = np.concatenate([r[c]["cv_s"] for c in range(8)], 1)[:, :, None, :]
    return (y_prompt.astype(np.float32), y_sample.astype(np.float32), hgrn_prompt.astype(np.float32), pool_prompt.astype(np.float32),
            hgrn_sample.astype(np.float32), pool_sample.astype(np.float32), chunk_v.astype(np.float32))
```
